# Optimizing a Trainium2 kernel written in Bass

```python
import math
import jax, jax.numpy as jnp
from jax import lax
import numpy as np

D_MODEL = 1024
BATCH = 8
SEQ = 2048
DEPTH = 1

HEAD_DIM = 64
ATTN_WIDTH = D_MODEL // 2
ATTN_HEADS = ATTN_WIDTH // HEAD_DIM
MOBA_BLOCK = 256
MOBA_TOPK = 3
Q_CHUNK = 32
POOL_WINDOWS = (2, 4, 8, 16)
POOL_GROUPS = len(POOL_WINDOWS)
POOL_WIDTH = D_MODEL // 2
POOL_GROUP_WIDTH = POOL_WIDTH // POOL_GROUPS
N_BRANCHES = 2
IN_WIDTH = 3 * ATTN_WIDTH + POOL_WIDTH + N_BRANCHES * D_MODEL
D_FF = 2816
RPB_BUCKETS = 32
RPB_MAX_DISTANCE = 128
RMS_EPS = 1e-6

kernel_name = "hybrid_moba_pool_macaron_block"


def rms_norm(x, g):
    x32 = x.astype(jnp.float32)
    y = x32 * lax.rsqrt(jnp.mean(x32 * x32, axis=-1, keepdims=True) + RMS_EPS)
    return (y * g.astype(jnp.float32)).astype(x.dtype)


def swiglu(x, w_gate, w_up, w_down):
    return (jax.nn.silu(x @ w_gate) * (x @ w_up)) @ w_down


def rpb_bucket(dist):
    n = jnp.maximum(dist, 0)
    max_exact = RPB_BUCKETS // 2
    nf = jnp.maximum(n, 1).astype(jnp.float32)
    large = max_exact + (jnp.log(nf / max_exact) / math.log(RPB_MAX_DISTANCE / max_exact)
                         * (RPB_BUCKETS - max_exact)).astype(jnp.int32)
    large = jnp.minimum(large, RPB_BUCKETS - 1)
    return jnp.where(n < max_exact, n, large)


def moba_attention(q, k, v, rpb_table):
    B, H, S, Dh = q.shape
    nb = -(-S // MOBA_BLOCK)
    s_pad = nb * MOBA_BLOCK
    pad = ((0, 0), (0, 0), (0, s_pad - S), (0, 0))
    k_pad = jnp.pad(k, pad)
    v_pad = jnp.pad(v, pad)
    k_blocks = k_pad.reshape(B, H, nb, MOBA_BLOCK, Dh)
    v_blocks = v_pad.reshape(B, H, nb, MOBA_BLOCK, Dh)
    k_mean = jnp.mean(k_blocks, axis=3)
    pos = jnp.arange(S, dtype=jnp.int32)
    q_blk = pos // MOBA_BLOCK
    gate = jnp.einsum('bhsd,bhnd->bhsn', q, k_mean).astype(jnp.float32)
    past = jnp.arange(nb, dtype=jnp.int32)[None, :] < q_blk[:, None]
    gate = jnp.where(past, gate, -jnp.inf)
    n_sel = min(MOBA_TOPK, nb)
    _, sel_idx = lax.top_k(gate, n_sel)
    table_hb = rpb_table.astype(jnp.float32).T
    scale = HEAD_DIM ** -0.5
    bi = jnp.arange(B)[:, None, None, None]
    hi = jnp.arange(H)[None, :, None, None]
    offs = jnp.arange(MOBA_BLOCK, dtype=jnp.int32)

    def chunk(c):
        t0 = c * Q_CHUNK
        ib = t0 // MOBA_BLOCK
        tq = t0 + jnp.arange(Q_CHUNK, dtype=jnp.int32)
        qc = lax.dynamic_slice_in_dim(q, t0, Q_CHUNK, axis=2)
        idx = lax.dynamic_slice_in_dim(sel_idx, t0, Q_CHUNK, axis=2)
        kg = k_blocks[bi, hi, idx]
        vg = v_blocks[bi, hi, idx]
        kpos_sel = idx[..., None] * MOBA_BLOCK + offs
        lg_sel = jnp.einsum('bhqd,bhqnkd->bhqnk', qc, kg).astype(jnp.float32) * scale
        lg_sel = lg_sel + table_hb[hi[..., None], rpb_bucket(tq[:, None, None] - kpos_sel)]
        valid = jnp.arange(n_sel, dtype=jnp.int32) < ib
        lg_sel = jnp.where(valid[:, None], lg_sel, -jnp.inf)
        ko = lax.dynamic_slice_in_dim(k_pad, ib * MOBA_BLOCK, MOBA_BLOCK, axis=2)
        vo = lax.dynamic_slice_in_dim(v_pad, ib * MOBA_BLOCK, MOBA_BLOCK, axis=2)
        dist = tq[:, None] - (ib * MOBA_BLOCK + offs)[None, :]
        lg_own = jnp.einsum('bhqd,bhkd->bhqk', qc, ko).astype(jnp.float32) * scale
        lg_own = lg_own + table_hb[:, rpb_bucket(dist)]
        lg_own = jnp.where(dist >= 0, lg_own, -jnp.inf)
        logits = jnp.concatenate([lg_sel.reshape(B, H, Q_CHUNK, n_sel * MOBA_BLOCK), lg_own], axis=-1)
        p = jax.nn.softmax(logits, axis=-1).astype(v.dtype)
        p_sel = p[..., :n_sel * MOBA_BLOCK].reshape(B, H, Q_CHUNK, n_sel, MOBA_BLOCK)
        p_own = p[..., n_sel * MOBA_BLOCK:]
        return (jnp.einsum('bhqnk,bhqnkd->bhqd', p_sel, vg)
                + jnp.einsum('bhqk,bhkd->bhqd', p_own, vo))

    outs = lax.map(chunk, jnp.arange(S // Q_CHUNK, dtype=jnp.int32))
    return outs.transpose(1, 2, 0, 3, 4).reshape(B, H, S, Dh)


def pool_mixer(z, w_group, ch_scale):
    B, S, _ = z.shape
    G = POOL_GROUP_WIDTH
    z32 = z.astype(jnp.float32)
    c_pad = jnp.concatenate([jnp.zeros((B, 1, POOL_WIDTH), jnp.float32),
                             jnp.cumsum(z32, axis=1)], axis=1)
    t1 = jnp.arange(1, S + 1, dtype=jnp.float32)
    outs = []
    for g, w in enumerate(POOL_WINDOWS):
        sl = slice(g * G, (g + 1) * G)
        cg = c_pad[:, :, sl]
        lag = jnp.concatenate([jnp.zeros((B, w - 1, G), jnp.float32), cg[:, :S - w + 1]], axis=1)
        mean = (cg[:, 1:] - lag) / jnp.minimum(t1, float(w))[None, :, None]
        outs.append(mean - z32[:, :, sl])
    pooled = jnp.stack(outs, axis=2).astype(z.dtype)
    mixed = jnp.einsum('bsgc,gcd->bsgd', pooled, w_group).reshape(B, S, POOL_WIDTH)
    return mixed * ch_scale


def setup_inputs(seed: int = 0) -> dict:
    key = jax.random.key(seed)
    ks = jax.random.split(key, 20)
    f32 = jnp.float32
    L = DEPTH

    def w(k, shape, fan_in):
        return jax.random.normal(k, shape, f32) * (fan_in ** -0.5)

    def gain(k, shape):
        return 1.0 + 0.02 * jax.random.normal(k, shape, f32)

    return {
        "x": jax.random.normal(ks[0], (BATCH, SEQ, D_MODEL), f32),
        "ffn1_norm": gain(ks[1], (L, D_MODEL)),
        "ffn1_w_gate": w(ks[2], (L, D_MODEL, D_FF), D_MODEL),
        "ffn1_w_up": w(ks[3], (L, D_MODEL, D_FF), D_MODEL),
        "ffn1_w_down": w(ks[4], (L, D_FF, D_MODEL), D_FF),
        "mix_norm": gain(ks[5], (L, D_MODEL)),
        "w_in": w(ks[6], (L, D_MODEL, IN_WIDTH), D_MODEL),
        "pool_w_group": w(ks[7], (L, POOL_GROUPS, POOL_GROUP_WIDTH, POOL_GROUP_WIDTH), POOL_GROUP_WIDTH),
        "pool_scale": gain(ks[8], (L, POOL_WIDTH)) + 0.08 * jax.random.normal(ks[9], (L, POOL_WIDTH), f32),
        "w_branch_attn": w(ks[10], (L, ATTN_WIDTH, D_MODEL), ATTN_WIDTH),
        "w_branch_pool": w(ks[11], (L, POOL_WIDTH, D_MODEL), POOL_WIDTH),
        "w_out": w(ks[12], (L, D_MODEL, D_MODEL), D_MODEL),
        "ffn2_norm": gain(ks[13], (L, D_MODEL)),
        "ffn2_w_gate": w(ks[14], (L, D_MODEL, D_FF), D_MODEL),
        "ffn2_w_up": w(ks[15], (L, D_MODEL, D_FF), D_MODEL),
        "ffn2_w_down": w(ks[16], (L, D_FF, D_MODEL), D_FF),
        "rpb_table": 0.5 * jax.random.normal(ks[17], (RPB_BUCKETS, ATTN_HEADS), f32),
        "final_norm": gain(ks[18], (D_MODEL,)),
    }


def reference(x, ffn1_norm, ffn1_w_gate, ffn1_w_up, ffn1_w_down, mix_norm, w_in,
              pool_w_group, pool_scale, w_branch_attn, w_branch_pool, w_out,
              ffn2_norm, ffn2_w_gate, ffn2_w_up, ffn2_w_down, rpb_table, final_norm):
    B, S, _ = x.shape
    h = x
    for l in range(DEPTH):
        h = h + 0.5 * swiglu(rms_norm(h, ffn1_norm[l]), ffn1_w_gate[l], ffn1_w_up[l], ffn1_w_down[l])
        u = rms_norm(h, mix_norm[l])
        proj = u @ w_in[l]
        q, k, v, z, g_attn, g_pool = jnp.split(
            proj, np.cumsum([ATTN_WIDTH, ATTN_WIDTH, ATTN_WIDTH, POOL_WIDTH, D_MODEL]).tolist(), axis=-1)
        to_heads = lambda t: t.reshape(B, S, ATTN_HEADS, HEAD_DIM).transpose(0, 2, 1, 3)
        a = moba_attention(to_heads(q), to_heads(k), to_heads(v), rpb_table)
        a = a.transpose(0, 2, 1, 3).reshape(B, S, ATTN_WIDTH)
        p = pool_mixer(z, pool_w_group[l], pool_scale[l])
        merged = (jax.nn.sigmoid(g_attn) * (a @ w_branch_attn[l])
                  + jax.nn.sigmoid(g_pool) * (p @ w_branch_pool[l]))
        h = h + merged @ w_out[l]
        h = h + 0.5 * swiglu(rms_norm(h, ffn2_norm[l]), ffn2_w_gate[l], ffn2_w_up[l], ffn2_w_down[l])
    return rms_norm(h, final_norm)
```

```python
import math
from contextlib import ExitStack

import numpy as np
import ml_dtypes

import concourse.bass as bass
import concourse.mybir as mybir
from concourse.bass_utils import run_bass_kernel_spmd

F32 = mybir.dt.float32
BF16 = mybir.dt.bfloat16
U8 = mybir.dt.uint8
AF = mybir.ActivationFunctionType
ALU = mybir.AluOpType
AX = mybir.AxisListType

D = 1024
S = 2048
NB = 8
DFF = 2816
NFC = DFF // 128
FH = 11
NDC = D // 128
NTT = S // 512
H = 8
HD = 64
BLK = 256
EPS = 1e-6
NEG = -30000.0
RING_SLOTS = 5
SLOT_BYTES = 4096
PF = 3
FW = 1152
TL = 1280


class Reg:
    __slots__ = ("name", "w", "r")

    def __init__(self, name):
        self.name = name
        self.w = None
        self.r = {}


class Prog:
    ENGS = ("pe", "act", "dve", "pool", "sp")

    def __init__(self):
        self.lists = {e: [] for e in self.ENGS}
        self.cnt = {e: 0 for e in self.ENGS}
        self.waited = {e: {} for e in self.ENGS}
        self.same_wait = {"pe": False, "act": True, "dve": True, "pool": True, "sp": False}
        self.psem = {e: "p_" + e for e in self.ENGS}
        self.dcnt = {}
        self.sem_names = set(self.psem.values())

    def _deps(self, reads, writes):
        deps = {}

        def add(t):
            if t is None:
                return
            s, v = t
            if deps.get(s, 0) < v:
                deps[s] = v

        for r in reads:
            add(r.w)
        for w in writes:
            add(w.w)
            for s, v in w.r.items():
                add((s, v))
        return deps

    def _wait(self, e, deps, skip=None):
        for s, v in deps.items():
            if s == skip:
                continue
            if s == self.psem[e] and not self.same_wait[e]:
                continue
            if self.waited[e].get(s, 0) < v:
                self.lists[e].append(("wait", s, v))
                self.waited[e][s] = v

    def _update(self, tok, reads, writes):
        s, v = tok
        for r in reads:
            if r.r.get(s, 0) < v:
                r.r[s] = v
        for w in writes:
            w.w = tok
            w.r = {}

    def emit(self, e, fns, reads=(), writes=()):
        if not isinstance(fns, (list, tuple)):
            fns = [fns]
        self._wait(e, self._deps(reads, writes))
        for f in fns[:-1]:
            self.lists[e].append(("ins", f, None, 0))
        self.cnt[e] += 1
        self.lists[e].append(("ins", fns[-1], self.psem[e], 1))
        tok = (self.psem[e], self.cnt[e])
        self._update(tok, reads, writes)
        return tok

    def dma(self, q, fn, sem, reads=(), writes=()):
        self.sem_names.add(sem)
        self._wait(q, self._deps(reads, writes), skip=sem)
        self.dcnt[sem] = self.dcnt.get(sem, 0) + 16
        self.lists[q].append(("ins", fn, sem, 16))
        tok = (sem, self.dcnt[sem])
        self._update(tok, reads, writes)
        return tok

    def barrier(self, engs=("pe", "act", "dve", "sp"), dma_sems=()):
        for e in engs:
            deps = {}
            for o in engs:
                if o != e and self.cnt[o] > 0:
                    deps[self.psem[o]] = self.cnt[o]
            for s in dma_sems:
                if self.dcnt.get(s, 0) > 0:
                    deps[s] = self.dcnt[s]
            self._wait(e, deps)

    def wait_all(self, e, sems):
        deps = {s: self.dcnt[s] for s in sems if self.dcnt.get(s, 0) > 0}
        self._wait(e, deps)


def _rpb_bucket_np(dist):
    n = np.maximum(dist, 0)
    max_exact = 16
    nf = np.maximum(n, 1).astype(np.float32)
    large = max_exact + (np.log(nf / np.float32(max_exact)) / np.float32(math.log(128 / max_exact))
                         * np.float32(32 - max_exact)).astype(np.int32)
    large = np.minimum(large, 31)
    return np.where(n < max_exact, n, large)


def _host_consts():
    oh = np.zeros((33, FW), np.float32)
    for i in range(FW):
        d = i - 511
        if d < 0 or i >= 1151:
            oh[32, i] = 1.0
        else:
            oh[int(_rpb_bucket_np(np.array([d], np.int32))[0]), i] = 1.0
    kblk = np.zeros((8, S), np.float32)
    for j in range(8):
        kblk[j, j * BLK:(j + 1) * BLK] = 1.0
    ssum = np.zeros((64, 4, 72), np.float32)
    for ib in range(4, 8):
        for j in range(ib):
            for jp in range(ib):
                ssum[j * 8 + jp, ib - 4, 64 + j] = 1.0
    rc = np.zeros((128, 16), np.float32)
    rc[:, :] = 1.0 / np.arange(1, 17, dtype=np.float32)[None, :]
    return {
        "c_onehot": oh,
        "c_kblk": kblk.astype(ml_dtypes.bfloat16),
        "c_ssum": ssum.reshape(64, 4 * 72).astype(ml_dtypes.bfloat16),
        "c_rc": rc,
    }


def build_program(stop_after="all"):
    nc = bass.Bass("TRN2", target_bir_lowering=False)
    P = Prog()

    def din(name, shape, dt=F32):
        return nc.dram_tensor(name, list(shape), dt, kind="ExternalInput").ap()

    xT = din("xT", [128, NDC, S])
    gains = din("gains", [128, 4 * NDC])
    pscale_d = din("pscale", [128, 4])
    rpb_d = din("rpb", [32, H])
    wgu_d = [din("wgu%d" % n, [NFC, 128, 2048]) for n in (1, 2)]
    wd_d = [din("wd%d" % n, [16, 128, FH * 128]) for n in (1, 2)]
    wv_d = din("wv", [2, 128, 2048])
    wz_d = din("wz", [4, 128, 1024])
    wgrp_d = din("wgrp", [4, 128, 128])
    wqk_d = din("wqk", [H, 128, 1024])
    wgate_d = din("wgate", [8, 128, 2048])
    wbr_d = din("wbr", [8, 128, 1024])
    wout_d = din("wout", [8, 128, 1024])
    c_onehot = din("c_onehot", [33, FW])
    c_kblk = din("c_kblk", [8, S], BF16)
    c_ssum = din("c_ssum", [64, 4 * 72], BF16)
    c_rc = din("c_rc", [128, 16])
    outT = nc.dram_tensor("outT", [128, NDC, S], F32, kind="ExternalOutput").ap()
    fd_t = nc.dram_tensor("fd_scr", [H, FW], BF16, kind="Internal")
    tsk_t = nc.dram_tensor("tsk_scr", [H, 128 * TL], BF16, kind="Internal")

    R1_BYTES = 49152
    SCR_BYTES = 16384
    off = {}
    cur = [0]

    def region(name, nbytes):
        off[name] = cur[0]
        cur[0] += (nbytes + 31) // 32 * 32

    region("h", 65536)
    region("u", 32768)
    region("r1", R1_BYTES)
    region("aT", 16384)
    region("ring", RING_SLOTS * SLOT_BYTES)
    region("rt", 2048)
    region("rstd", 2048)
    region("sq", 3 * 1024)
    region("scr", SCR_BYTES)
    region("gains", 128)
    region("pscale", 32)
    region("onesD", 256)
    region("ones64", 128)
    region("cbias", 32)
    region("rc", 64)
    region("tab33", 32)
    region("ssum", 576)
    TOTAL = cur[0]
    big = nc.alloc_sbuf_tensor("big", [128, TOTAL], U8)

    def view(base, nbytes, dt, pattern=None, **kw):
        v = big[:, base:base + nbytes].bitcast(dt)
        if pattern:
            v = v.rearrange(pattern, **kw)
        return v

    h = view(off["h"], 65536, F32, "p (a t) -> p a t", a=NDC)
    u = view(off["u"], 32768, BF16, "p (a t) -> p a t", a=NDC)
    r1 = off["r1"]
    A = view(r1, FH * 4096, BF16, "p (a t) -> p a t", a=FH)
    pT = view(r1, 16384, BF16, "p (a t) -> p a t", a=4)
    Va = view(r1 + 16384, 16640, BF16, "p (i c) -> p i c", i=16)
    QaB = [view(r1 + 33024, 4096, BF16), view(r1, 4096, BF16)]
    KaB = [view(r1 + 37120, 4096, BF16), view(r1 + 4096, 4096, BF16)]
    G2 = [view(r1 + 41216 + i * 2048, 2048, BF16) for i in range(2)]
    merged = view(r1 + 16384, 32768, BF16, "p (a t) -> p a t", a=NDC)
    of32 = [view(r1 + i * 16384, 16384, F32, "p (a t) -> p a t", a=NDC) for i in range(2)]
    aT = view(off["aT"], 16384, BF16, "p (a t) -> p a t", a=4)
    zf = view(r1 + 16384, 8192, F32)
    ta = view(r1 + 24576, 8192, F32)
    tb = view(r1 + 32768, 8192, F32)
    ta_bf = view(r1 + 24576, 4096, BF16)
    tb_bf = view(r1 + 32768, 4096, BF16)
    ring = [view(off["ring"] + i * SLOT_BYTES, SLOT_BYTES, BF16) for i in range(RING_SLOTS)]
    rt = view(off["rt"], 2048, F32)
    rstd = view(off["rstd"], 2048, F32)
    sq = [view(off["sq"] + i * 1024, 1024, BF16) for i in range(3)]
    scr = off["scr"]
    gains_sb = view(off["gains"], 128, F32)
    pscale_sb = view(off["pscale"], 16, F32)
    onesD = view(off["onesD"], 256, BF16)
    ones64 = view(off["ones64"], 128, BF16)
    cbias = view(off["cbias"], 32, F32)
    rc_sb = view(off["rc"], 64, F32)
    tab33 = view(off["tab33"], 32, F32)
    ssum_sb = view(off["ssum"], 576, BF16, "p (a c) -> p a c", a=4)
    sg = [view(scr + i * 1024, 1024, BF16) for i in range(3)]
    onehot_sb = view(off["aT"], FW * 4, F32)
    fexp_sb = view(off["aT"] + FW * 4, FW * 2, BF16)
    Pt = [view(scr + i * 1024, 1024, BF16) for i in range(6)]
    Et = [view(scr + 6144 + i * 1024, 1024, BF16) for i in range(2)]
    dhi = [view(scr + 8192 + i * 1024, 1024, BF16) for i in range(2)]
    dlo = [view(scr + 10240 + i * 1024, 1024, BF16) for i in range(2)]
    rec = [view(scr + 12288, 2048, F32)]
    stg = [view(scr + 14336 + i * 1024, 1024, BF16) for i in range(2)]
    ind = [view(r1 + 45312 + i * 1024, 1024, BF16) for i in range(2)]
    kmT = view(r1 + 47360, 32, F32)
    diffT = view(r1 + 47392, 128, BF16)
    tmp16 = view(scr, 64, F32)
    sgm = [view(scr + i * 2048, 2048, F32) for i in range(2)]
    m1 = [view(scr + 4096 + i * 2048, 2048, F32) for i in range(2)]

    bank = [nc.alloc_psum_tensor("bank%d" % i, [128, 512], F32) for i in range(8)]
    bank_reg = [Reg("bank%d" % i) for i in range(8)]

    h_reg = [[Reg("h%d_%d" % (dc, tt)) for tt in range(NTT)] for dc in range(NDC)]
    u_reg = [Reg("u%d" % tt) for tt in range(NTT)]
    A_reg = [[Reg("A%d_%d" % (f, tt)) for tt in range(NTT)] for f in range(FH)]
    sq_reg = [Reg("sq%d" % i) for i in range(3)]
    rt_reg = Reg("rt")
    rstd_reg = Reg("rstd")
    sg_reg = [Reg("sg%d" % i) for i in range(3)]
    slot_reg = [Reg("slot%d" % i) for i in range(RING_SLOTS)]
    const_reg = Reg("consts")
    G_reg = Reg("G")

    slabs = []

    def ffn_slabs(n):
        for fh in range(2):
            for fcl in range(FH):
                slabs.append([(wgu_d[n][fh * FH + fcl], 0, 2048)])
            for dco in range(NDC):
                slabs.append([(wd_d[n][fh * NDC + dco], 0, FH * 128)])

    ffn_slabs(0)
    slabs.append([(wv_d[0], 0, 2048)])
    slabs.append([(wv_d[1], 0, 2048)])
    for hh in range(H):
        slabs.append([(wqk_d[hh], 0, 1024)])
    for g in range(4):
        slabs.append([(wz_d[g], 0, 1024), (wgrp_d[g], 1024, 128)])
    for c in range(8):
        slabs.append([(wgate_d[c], 0, 2048)])
        slabs.append([(wbr_d[c], 0, 1024)])
    for c in range(8):
        slabs.append([(wout_d[c], 0, 1024)])
    ffn_slabs(1)
    state = {"issued": 0, "next": 0}

    def issue_slabs(upto):
        while state["issued"] < min(upto, len(slabs)):
            i = state["issued"]
            s = i % RING_SLOTS
            for (src, o, n) in slabs[i]:
                dst = ring[s][:, o:o + n]
                P.dma("pool", (lambda g, dst=dst, src=src: g.dma_start(out=dst, in_=src)),
                      "ring%d" % s, writes=[slot_reg[s]])
            state["issued"] += 1

    def next_slab():
        i = state["next"]
        state["next"] += 1
        issue_slabs(i + PF + 1)
        s = i % RING_SLOTS
        return ring[s], slot_reg[s]

    def mm(out, lhsT, rhs, start, stop):
        return lambda t: t.matmul(out, lhsT, rhs, start=start, stop=stop)

    sqi = [0]

    def rmsnorm(n, tt, dst_fn, dst_regs, ssq_bank):
        ts = slice(tt * 512, (tt + 1) * 512)
        for dc in range(NDC):
            k = sqi[0] % 3
            sqi[0] += 1
            P.emit("act", (lambda a, k=k, dc=dc: a.activation(out=sq[k][:, :], in_=h[:, dc, ts], func=AF.Square)),
                   reads=[h_reg[dc][tt]], writes=[sq_reg[k]])
            P.emit("pe", mm(bank[ssq_bank][:, :], onesD[:, :], sq[k][:, :], dc == 0, dc == NDC - 1),
                   reads=[sq_reg[k]], writes=[bank_reg[ssq_bank]])
        P.emit("act", (lambda a: a.activation(out=rt[:, :], in_=bank[ssq_bank][:, :], func=AF.Sqrt, bias=EPS, scale=1.0)),
               writes=[bank_reg[ssq_bank], rt_reg])
        P.emit("dve", (lambda v: v.reciprocal(out=rstd[:, :], in_=rt[:, :])), reads=[rt_reg], writes=[rstd_reg])
        for dc in range(NDC):
            gcol = gains_sb[:, n * NDC + dc:n * NDC + dc + 1]
            P.emit("dve", (lambda v, dc=dc, gcol=gcol: v.scalar_tensor_tensor(
                out=dst_fn(dc), in0=h[:, dc, ts], scalar=gcol, in1=rstd[:, :], op0=ALU.mult, op1=ALU.mult)),
                reads=[h_reg[dc][tt], rstd_reg, const_reg], writes=dst_regs(dc))

    def ffn(n, gain_idx):
        for tt in range(NTT):
            ts = slice(tt * 512, (tt + 1) * 512)
            rmsnorm(gain_idx, tt, (lambda dc, ts=ts: u[:, dc, ts]), (lambda dc, tt=tt: [u_reg[tt]]), 6 + tt % 2)
        k = 0
        kk = 0
        for fh in range(2):
            for fcl in range(FH):
                slab, sreg = next_slab()
                wg = slab[:, 0:1024].rearrange("p (a c) -> p a c", a=NDC)
                wu = slab[:, 1024:2048].rearrange("p (a c) -> p a c", a=NDC)
                for tt in range(NTT):
                    ts = slice(tt * 512, (tt + 1) * 512)
                    gb = k % 2
                    ub = 2 + k % 2
                    si = k % 3
                    k += 1
                    P.emit("pe", [mm(bank[gb][:, :], wg[:, dc, :], u[:, dc, ts], dc == 0, dc == NDC - 1) for dc in range(NDC)],
                           reads=[u_reg[tt], sreg], writes=[bank_reg[gb]])
                    P.emit("pe", [mm(bank[ub][:, :], wu[:, dc, :], u[:, dc, ts], dc == 0, dc == NDC - 1) for dc in range(NDC)],
                           reads=[u_reg[tt], sreg], writes=[bank_reg[ub]])
                    P.emit("act", (lambda a, gb=gb, si=si: a.activation(out=sg[si][:, :], in_=bank[gb][:, :], func=AF.Silu)),
                           writes=[bank_reg[gb], sg_reg[si]])
                    P.emit("dve", (lambda v, ub=ub, si=si, fcl=fcl, ts=ts: v.tensor_tensor(
                        out=A[:, fcl, ts], in0=bank[ub][:, :], in1=sg[si][:, :], op=ALU.mult)),
                        reads=[sg_reg[si]], writes=[bank_reg[ub], A_reg[fcl][tt]])
            for dco in range(NDC):
                slab, sreg = next_slab()
                wdv = slab[:, 0:FH * 128].rearrange("p (a c) -> p a c", a=FH)
                for tt in range(NTT):
                    ts = slice(tt * 512, (tt + 1) * 512)
                    yb = 4 + kk % 2
                    kk += 1
                    P.emit("pe", [mm(bank[yb][:, :], wdv[:, f, :], A[:, f, ts], f == 0, f == FH - 1) for f in range(FH)],
                           reads=[A_reg[f][tt] for f in range(FH)] + [sreg], writes=[bank_reg[yb]])
                    P.emit("dve", (lambda v, yb=yb, dco=dco, ts=ts: v.scalar_tensor_tensor(
                        out=h[:, dco, ts], in0=bank[yb][:, :], scalar=0.5, in1=h[:, dco, ts], op0=ALU.mult, op1=ALU.add)),
                        writes=[bank_reg[yb], h_reg[dco][tt]])

    for tt in range(NTT):
        ts = slice(tt * 512, (tt + 1) * 512)
        P.dma("sp", (lambda q, ts=ts: q.dma_start(out=h[:, :, ts], in_=xT[:, :, ts])), "x%d" % tt,
              writes=[h_reg[dc][tt] for dc in range(NDC)])
    for (dst, src) in (
        (gains_sb[:, :], gains),
        (pscale_sb[:, :], pscale_d),
        (rc_sb[:, :], c_rc),
        (ssum_sb[0:64, :, :], c_ssum.rearrange("p (a c) -> p a c", a=4)),
        (tab33[0:32, :], rpb_d),
        (cbias[:, :], rpb_d[31:32, :].partition_broadcast(128)),
        (onehot_sb[0:33, :], c_onehot),
    ):
        P.dma("sp", (lambda q, dst=dst, src=src: q.dma_start(out=dst, in_=src)), "cst", writes=[const_reg])
    P.emit("dve", (lambda v: v.memset(onesD[:, :], 1.0 / D)), writes=[const_reg])
    P.emit("dve", (lambda v: v.memset(ones64[:, :], 1.0)), writes=[const_reg])
    P.emit("dve", (lambda v: v.memset(tab33[32:33, :], NEG)), writes=[const_reg])
    for e in ("pe", "act", "dve"):
        P.wait_all(e, ["cst"])
    P.barrier(engs=("pe", "act", "dve"))

    fb = 0
    for (c0, c1) in ((0, 512), (512, 1024), (1024, FW)):
        P.emit("pe", mm(bank[fb][0:8, 0:c1 - c0], tab33[0:33, 0:8], onehot_sb[0:33, c0:c1], True, True), writes=[bank_reg[fb]])
        P.emit("act", (lambda a, c0=c0, c1=c1: a.activation(out=fexp_sb[0:8, c0:c1], in_=bank[fb][0:8, 0:c1 - c0], func=AF.Exp)),
               writes=[bank_reg[fb], G_reg])
    P.dma("sp", (lambda q: q.dma_start(out=fd_t.ap(), in_=fexp_sb[0:8, :])), "g1", reads=[G_reg], writes=[G_reg])
    for hh in range(H):
        src = bass.AP(fd_t, hh * FW, [[0, 128], [1, 1151]])
        dst = bass.AP(tsk_t, hh * 128 * TL, [[TL + 1, 128], [1, 1151]])
        P.dma("sp", (lambda q, dst=dst, src=src: q.dma_start(out=dst, in_=src)), "g2", reads=[G_reg], writes=[])
    G2_reg = Reg("G2")
    G2_reg.w = ("g2", P.dcnt["g2"])
    Gh_reg = [Reg("Gh0"), Reg("Gh1")]

    def load_G(hh):
        src = bass.AP(tsk_t, hh * 128 * TL + 127, [[TL, 128], [1, 1024]])
        P.dma("sp", (lambda q: q.dma_start(out=G2[hh % 2][:, :], in_=src)), "gh%d" % (hh % 2),
              reads=[G2_reg], writes=[Gh_reg[hh % 2]])

    def out_h_raw():
        for tt in range(NTT):
            ts = slice(tt * 512, (tt + 1) * 512)
            P.dma("sp", (lambda q, ts=ts: q.dma_start(out=outT[:, :, ts], in_=h[:, :, ts])), "st",
                  reads=[h_reg[dc][tt] for dc in range(NDC)])
        P.wait_all("sp", ["st"])

    def done():
        replay(nc, P)
        return nc

    ffn(0, 0)
    if stop_after == "ffn1":
        out_h_raw()
        return done()

    P.barrier()
    for tt in range(NTT):
        ts = slice(tt * 512, (tt + 1) * 512)
        rmsnorm(1, tt, (lambda dc, ts=ts: u[:, dc, ts]), (lambda dc, tt=tt: [u_reg[tt]]), 6 + tt % 2)

    Va4 = Va.rearrange("p i (h c) -> p i h c", c=65)
    Va_reg = [Reg("Va%d" % i) for i in range(16)]
    s0, sr0 = next_slab()
    s1, sr1 = next_slab()
    wv = [s0.rearrange("p (a c) -> p a c", a=4), s1.rearrange("p (a c) -> p a c", a=4)]
    P.emit("dve", (lambda v: v.memset(Va4[:, :, :, 64:65], 1.0)), writes=Va_reg)
    for i in range(16):
        b = i % 4
        tsl = slice(i * 128, (i + 1) * 128)
        P.emit("pe", [mm(bank[b][:, :], u[:, dc, tsl], wv[dc // 4][:, dc % 4, :], dc == 0, dc == NDC - 1) for dc in range(NDC)],
               reads=[u_reg[i // 4], sr0, sr1], writes=[bank_reg[b]])
        src = bank[b][:, :].rearrange("p (h c) -> p h c", c=64)
        if i % 2 == 0:
            P.emit("act", (lambda a, i=i, src=src: a.activation(out=Va4[:, i, :, 0:64], in_=src, func=AF.Copy)),
                   writes=[bank_reg[b], Va_reg[i]])
        else:
            P.emit("dve", (lambda v, i=i, src=src: v.tensor_copy(out=Va4[:, i, :, 0:64], in_=src)),
                   writes=[bank_reg[b], Va_reg[i]])

    Kb_reg = [Reg("Kb0"), Reg("Kb1")]
    Qz_reg = [Reg("Qz0"), Reg("Qz1")]
    Qm_reg = [[Reg("Qm%d_%d" % (b, ib)) for ib in range(8)] for b in range(2)]
    Qa_reg = [[Reg("Qa%d_%d" % (b, tt)) for tt in range(NTT)] for b in range(2)]
    Ka_reg = [[Reg("Ka%d_%d" % (b, tt)) for tt in range(NTT)] for b in range(2)]
    km_reg = Reg("km")
    df_reg = Reg("df")
    ind_reg = [Reg("ind0"), Reg("ind1")]
    Pt_reg = [Reg("Pt%d" % i) for i in range(6)]
    Et_reg = [Reg("Et%d" % i) for i in range(2)]
    dhi_reg = [Reg("dhi%d" % i) for i in range(2)]
    dlo_reg = [Reg("dlo%d" % i) for i in range(2)]
    rec_reg = [Reg("rec0")]
    stg_reg = [Reg("stg%d" % i) for i in range(2)]
    aT_reg = [[Reg("aT%d_%d" % (hh, m)) for m in range(NTT)] for hh in range(H)]
    for b in range(2):
        P.dma("sp", (lambda q, b=b: q.dma_start(out=KaB[b][64:72, :], in_=c_kblk)), "cst2", writes=[Kb_reg[b]])
        P.emit("dve", (lambda v, b=b: v.memset(QaB[b][64:72, :], 0.0)), writes=[Qz_reg[b]] + Qm_reg[b])
    cnts = {"s": 0, "p": 0, "e": 0, "f": 0, "g": 0, "o": 0}
    SCALE = HD ** -0.5
    load_G(0)

    def proj_groups(hh):
        b = hh % 2
        slab, sreg = next_slab()
        wq = slab[:, 0:512].rearrange("p (a c) -> p a c", a=NDC)
        wk = slab[:, 512:1024].rearrange("p (a c) -> p a c", a=NDC)
        out = []
        for tt in range(NTT):
            ts = slice(tt * 512, (tt + 1) * 512)
            for (wm, dstb, dreg) in ((wq, QaB[b], Qa_reg[b][tt]), (wk, KaB[b], Ka_reg[b][tt])):
                def grp(wm=wm, dstb=dstb, dreg=dreg, ts=ts, tt=tt):
                    P.emit("pe", [mm(bank[0][0:64, :], wm[:, dc, :], u[:, dc, ts], dc == 0, dc == NDC - 1) for dc in range(NDC)],
                           reads=[u_reg[tt], sreg], writes=[bank_reg[0]])
                    P.emit("dve", (lambda v: v.tensor_copy(out=dstb[0:64, ts], in_=bank[0][0:64, :])),
                           writes=[bank_reg[0], dreg])
                out.append(grp)
        return out

    def attn_head(hh, nxt):
        b = hh % 2
        Qa, Ka = QaB[b], KaB[b]
        if hh + 1 < H:
            load_G(hh + 1)
        Gt = G2[hh % 2]
        Greg = Gh_reg[hh % 2]
        P.emit("dve", (lambda v: v.reduce_sum(out=kmT[0:64, 0:8], in_=Ka[0:64, :].rearrange("p (j k) -> p j k", k=BLK), axis=AX.X)),
               reads=Ka_reg[b], writes=[km_reg])
        P.emit("dve", (lambda v: v.tensor_tensor(
            out=diffT[0:64, :].rearrange("p (a b) -> p a b", b=8),
            in0=kmT[0:64, 0:8].unsqueeze(1).broadcast_to([64, 8, 8]),
            in1=kmT[0:64, 0:8].unsqueeze(2).broadcast_to([64, 8, 8]), op=ALU.subtract)),
            reads=[km_reg], writes=[df_reg])

        def selection():
            for half in range(2):
                qs = slice(1024 + half * 512, 1536 + half * 512)
                P.emit("pe", mm(bank[1][0:64, :], diffT[0:64, 0:64], Qa[0:64, qs], True, True),
                       reads=[df_reg, Qa_reg[b][2 + half]], writes=[bank_reg[1]])
                P.emit("dve", (lambda v, half=half: v.tensor_single_scalar(
                    out=ind[half][0:64, :], in_=bank[1][0:64, :], scalar=0.0, op=ALU.is_gt)),
                    writes=[bank_reg[1], ind_reg[half]])
                for blk in range(2):
                    ib = 4 + half * 2 + blk
                    P.emit("pe", mm(bank[1][0:72, 0:256], ssum_sb[0:64, ib - 4, :], ind[half][0:64, blk * 256:(blk + 1) * 256], True, True),
                           reads=[ind_reg[half]], writes=[bank_reg[1]])
                    P.emit("dve", (lambda v, ib=ib: v.tensor_scalar(
                        out=Qa[64:72, ib * BLK:(ib + 1) * BLK], in0=bank[1][64:72, 0:256],
                        scalar1=2.5, scalar2=NEG, op0=ALU.is_gt, op1=ALU.mult)),
                        writes=[bank_reg[1], Qm_reg[b][ib]])

        tiles = [(m, kt) for m in range(NTT) for kt in range(4 * m + 4)]
        info = {}
        pending = []
        DEPTH = 4
        DELAY = 4
        SRING = (3, 4, 5, 2)

        def s_stage(idx):
            m, kt = tiles[idx]
            q0 = m * 512
            sb = SRING[cnts["s"] % len(SRING)]
            cnts["s"] += 1
            pi = cnts["p"] % 6
            cnts["p"] += 1
            info[idx] = pi
            delta = kt * 128 - q0
            cs = max(0, delta)
            rd = [Ka_reg[b][kt // 4], Kb_reg[b], Qa_reg[b][m]] + ([Qz_reg[b]] if m < 2 else [Qm_reg[b][2 * m], Qm_reg[b][2 * m + 1]])
            P.emit("pe", mm(bank[sb][:, cs:512], Ka[0:72, kt * 128:(kt + 1) * 128], Qa[0:72, q0 + cs:q0 + 512], True, True),
                   reads=rd, writes=[bank_reg[sb]])
            if delta < -128:
                P.emit("act", (lambda a, sb=sb, pi=pi: a.activation(
                    out=Pt[pi][:, :], in_=bank[sb][:, :], func=AF.Exp, bias=cbias[:, hh:hh + 1], scale=SCALE)),
                    writes=[bank_reg[sb], Pt_reg[pi]])
            else:
                ei = cnts["e"] % 2
                cnts["e"] += 1
                goff = 384 - delta
                P.emit("act", (lambda a, sb=sb, ei=ei, cs=cs: a.activation(
                    out=Et[ei][:, cs:512], in_=bank[sb][:, cs:512], func=AF.Exp, scale=SCALE)),
                    writes=[bank_reg[sb], Et_reg[ei]])
                eng = "pool" if cnts["o"] % 2 == 1 else "dve"
                cnts["o"] += 1
                P.emit(eng, (lambda g, ei=ei, pi=pi, goff=goff, cs=cs: g.tensor_tensor(
                    out=Pt[pi][:, cs:512], in0=Et[ei][:, cs:512], in1=Gt[:, goff + cs:goff + 512], op=ALU.mult)),
                    reads=[Et_reg[ei], Greg], writes=[Pt_reg[pi]])

        def pv_stage(idx):
            m, kt = tiles[idx]
            q0 = m * 512
            pi = info[idx]
            ob = 6 + m % 2
            last = 4 * m + 3
            cs = max(0, kt * 128 - q0)
            P.emit("pe", mm(bank[ob][0:65, cs:512], Va4[:, kt, hh, :], Pt[pi][:, cs:512], kt == 0, kt == last),
                   reads=[Va_reg[kt], Pt_reg[pi]], writes=[bank_reg[ob]])
            if kt != last:
                return
            fi = cnts["f"] % 2
            cnts["f"] += 1
            P.emit("dve", (lambda v, ob=ob, fi=fi: v.tensor_copy(out=dhi[fi][64:65, :], in_=bank[ob][64:65, :])),
                   writes=[bank_reg[ob], dhi_reg[fi]])
            P.emit("dve", (lambda v, ob=ob, fi=fi: v.tensor_tensor(
                out=dlo[fi][64:65, :], in0=bank[ob][64:65, :], in1=dhi[fi][64:65, :], op=ALU.subtract)),
                reads=[dhi_reg[fi]], writes=[bank_reg[ob], dlo_reg[fi]])

            def stage_b(ob=ob, fi=fi, m=m, q0=q0):
                P.emit("pe", [mm(bank[1][0:64, :], ones64[64:65, 0:64], dhi[fi][64:65, :], True, False),
                              mm(bank[1][0:64, :], ones64[64:65, 0:64], dlo[fi][64:65, :], False, True)],
                       reads=[dhi_reg[fi], dlo_reg[fi]], writes=[bank_reg[1]])
                P.emit("dve", (lambda v: v.reciprocal(out=rec[0][0:64, :], in_=bank[1][0:64, :])),
                       writes=[bank_reg[1], rec_reg[0]])
                c = hh // 2
                if hh % 2 == 0:
                    P.emit("dve", (lambda v: v.tensor_tensor(
                        out=aT[0:64, c, q0:q0 + 512], in0=bank[ob][0:64, :], in1=rec[0][0:64, :], op=ALU.mult)),
                        reads=[rec_reg[0]], writes=[bank_reg[ob], aT_reg[hh][m]])
                else:
                    gi = cnts["g"] % 2
                    cnts["g"] += 1
                    P.emit("dve", (lambda v: v.tensor_tensor(
                        out=stg[gi][0:64, :], in0=bank[ob][0:64, :], in1=rec[0][0:64, :], op=ALU.mult)),
                        reads=[rec_reg[0]], writes=[bank_reg[ob], stg_reg[gi]])
                    P.dma("sp", (lambda q: q.dma_start(out=aT[64:128, c, q0:q0 + 512], in_=stg[gi][0:64, :])),
                          "stg%d" % gi, reads=[stg_reg[gi]], writes=[aT_reg[hh][m]])

            pending.append([DELAY, stage_b])

        n = len(tiles)
        for idx in range(n + DEPTH):
            if idx == 4:
                selection()
            if idx < n:
                s_stage(idx)
            if idx - DEPTH >= 0:
                pv_stage(idx - DEPTH)
            if nxt and idx >= 6 and (idx - 6) % 4 == 0:
                nxt.pop(0)()
            for pb in list(pending):
                pb[0] -= 1
                if pb[0] <= 0:
                    pending.remove(pb)
                    pb[1]()
        for pb in pending:
            pb[1]()
        del pending[:]
        while nxt:
            nxt.pop(0)()

    cur = proj_groups(0)
    for g in cur:
        g()
    for hh in range(H):
        nxt = proj_groups(hh + 1) if hh + 1 < H else []
        attn_head(hh, nxt)
    if stop_after == "m3":
        for c in range(4):
            for tt in range(NTT):
                ts = slice(tt * 512, (tt + 1) * 512)
                P.emit("act", (lambda a, c=c, ts=ts: a.activation(out=h[:, c, ts], in_=aT[:, c, ts], func=AF.Copy)),
                       reads=[aT_reg[hh][tt] for hh in (2 * c, 2 * c + 1)],
                       writes=[h_reg[c][tt]])
        out_h_raw()
        return done()

    P.barrier(dma_sems=("stg0", "stg1"))
    zf_reg = [Reg("zf%d" % tt) for tt in range(NTT)]
    ta_reg = Reg("ta")
    tb_reg = Reg("tb")
    t16_reg = Reg("t16")
    pT_reg = [[Reg("pT%d_%d" % (g, tt)) for tt in range(NTT)] for g in range(4)]
    zslab = {}

    def z_proj(g):
        slab, sreg = next_slab()
        zslab[g] = (slab, sreg)
        wzv = slab[:, 0:1024].rearrange("p (a c) -> p a c", a=NDC)
        for tt in range(NTT):
            ts = slice(tt * 512, (tt + 1) * 512)
            b = (g % 2) * 4 + tt
            P.emit("pe", [mm(bank[b][:, :], wzv[:, dc, :], u[:, dc, ts], dc == 0, dc == NDC - 1) for dc in range(NDC)],
                   reads=[u_reg[tt], sreg], writes=[bank_reg[b]])

    z_proj(0)
    for g in range(4):
        w = 2 ** (g + 1)
        if g < 3:
            z_proj(g + 1)
        slab, sreg = zslab[g]
        wgg = slab[:, 1024:1152]
        for tt in range(NTT):
            ts = slice(tt * 512, (tt + 1) * 512)
            b = (g % 2) * 4 + tt
            P.emit("act", (lambda a, b=b, ts=ts: a.activation(out=zf[:, ts], in_=bank[b][:, :], func=AF.Copy)),
                   writes=[bank_reg[b], zf_reg[tt]])
        bufs = [(ta, ta_reg, ta_bf), (tb, tb_reg, tb_bf)]
        src, src_regs = zf, list(zf_reg)
        sft = 1
        idx = 0
        while sft < w:
            dst, dreg, _ = bufs[idx % 2]
            P.emit("dve", (lambda v, dst=dst, src=src, sft=sft: v.tensor_tensor(
                out=dst[:, sft:S], in0=src[:, sft:S], in1=src[:, 0:S - sft], op=ALU.add)),
                reads=src_regs, writes=[dreg])
            P.emit("dve", (lambda v, dst=dst, src=src, sft=sft: v.tensor_copy(out=dst[:, 0:sft], in_=src[:, 0:sft])),
                   reads=src_regs, writes=[dreg])
            src, src_regs = dst, [dreg]
            idx += 1
            sft *= 2
        fin, fin_regs = src, src_regs
        _, oreg, pooled = bufs[idx % 2]
        P.emit("dve", (lambda v, fin=fin, pooled=pooled, w=w: v.scalar_tensor_tensor(
            out=pooled[:, :], in0=fin[:, :], scalar=1.0 / w, in1=zf[:, :], op0=ALU.mult, op1=ALU.subtract)),
            reads=fin_regs + zf_reg, writes=[oreg])
        P.emit("dve", (lambda v, fin=fin, w=w: v.tensor_tensor(
            out=tmp16[:, 0:w - 1], in0=fin[:, 0:w - 1], in1=rc_sb[:, 0:w - 1], op=ALU.mult)),
            reads=fin_regs, writes=[t16_reg])
        P.emit("dve", (lambda v, pooled=pooled, w=w: v.tensor_tensor(
            out=pooled[:, 0:w - 1], in0=tmp16[:, 0:w - 1], in1=zf[:, 0:w - 1], op=ALU.subtract)),
            reads=[t16_reg] + zf_reg, writes=[oreg])
        for tt in range(NTT):
            ts = slice(tt * 512, (tt + 1) * 512)
            b = (g % 2) * 4 + tt
            P.emit("pe", mm(bank[b][:, :], wgg, pooled[:, ts], True, True), reads=[oreg, sreg], writes=[bank_reg[b]])
            P.emit("dve", (lambda v, b=b, g=g, ts=ts: v.tensor_single_scalar(
                out=pT[:, g, ts], in_=bank[b][:, :], scalar=pscale_sb[:, g:g + 1], op=ALU.mult)),
                writes=[bank_reg[b], pT_reg[g][tt]])

    P.barrier(dma_sems=("stg0", "stg1"))
    mg_reg = [[Reg("mg%d_%d" % (c, tt)) for tt in range(NTT)] for c in range(NDC)]
    sgm_reg = [Reg("sgm0"), Reg("sgm1")]
    m1_reg = [Reg("m10"), Reg("m11")]
    k4 = 0
    for c in range(NDC):
        slabg, sgr = next_slab()
        gaw = slabg[:, 0:1024].rearrange("p (a c) -> p a c", a=NDC)
        gpw = slabg[:, 1024:2048].rearrange("p (a c) -> p a c", a=NDC)
        slabb, sbr = next_slab()
        waw = slabb[:, 0:512].rearrange("p (a c) -> p a c", a=4)
        wpw = slabb[:, 512:1024].rearrange("p (a c) -> p a c", a=4)
        for tt in range(NTT):
            ts = slice(tt * 512, (tt + 1) * 512)
            b0 = 4 * (k4 % 2)
            k4 += 1
            a_rd = [aT_reg[hh][tt] for hh in range(H)]
            P.emit("pe", [mm(bank[b0][:, :], gaw[:, dc, :], u[:, dc, ts], dc == 0, dc == NDC - 1) for dc in range(NDC)],
                   reads=[u_reg[tt], sgr], writes=[bank_reg[b0]])
            P.emit("pe", [mm(bank[b0 + 1][:, :], waw[:, cc, :], aT[:, cc, ts], cc == 0, cc == 3) for cc in range(4)],
                   reads=a_rd + [sbr], writes=[bank_reg[b0 + 1]])
            P.emit("pe", [mm(bank[b0 + 2][:, :], gpw[:, dc, :], u[:, dc, ts], dc == 0, dc == NDC - 1) for dc in range(NDC)],
                   reads=[u_reg[tt], sgr], writes=[bank_reg[b0 + 2]])
            P.emit("pe", [mm(bank[b0 + 3][:, :], wpw[:, g, :], pT[:, g, ts], g == 0, g == 3) for g in range(4)],
                   reads=[pT_reg[g][tt] for g in range(4)] + [sbr], writes=[bank_reg[b0 + 3]])
            P.emit("act", (lambda a, b0=b0: a.activation(out=sgm[0][:, :], in_=bank[b0][:, :], func=AF.Sigmoid)),
                   writes=[bank_reg[b0], sgm_reg[0]])
            P.emit("act", (lambda a, b0=b0: a.activation(out=sgm[1][:, :], in_=bank[b0 + 2][:, :], func=AF.Sigmoid)),
                   writes=[bank_reg[b0 + 2], sgm_reg[1]])
            P.emit("dve", (lambda v, b0=b0: v.tensor_tensor(out=m1[0][:, :], in0=bank[b0 + 1][:, :], in1=sgm[0][:, :], op=ALU.mult)),
                   reads=[sgm_reg[0]], writes=[bank_reg[b0 + 1], m1_reg[0]])
            P.emit("dve", (lambda v, b0=b0: v.tensor_tensor(out=m1[1][:, :], in0=bank[b0 + 3][:, :], in1=sgm[1][:, :], op=ALU.mult)),
                   reads=[sgm_reg[1]], writes=[bank_reg[b0 + 3], m1_reg[1]])
            P.emit("dve", (lambda v, c=c, ts=ts: v.tensor_tensor(out=merged[:, c, ts], in0=m1[0][:, :], in1=m1[1][:, :], op=ALU.add)),
                   reads=[m1_reg[0], m1_reg[1]], writes=[mg_reg[c][tt]])
    ko = 0
    for co in range(NDC):
        slab, sreg = next_slab()
        wo = slab[:, 0:1024].rearrange("p (a c) -> p a c", a=NDC)
        for tt in range(NTT):
            ts = slice(tt * 512, (tt + 1) * 512)
            b = ko % 2
            ko += 1
            P.emit("pe", [mm(bank[b][:, :], wo[:, c, :], merged[:, c, ts], c == 0, c == NDC - 1) for c in range(NDC)],
                   reads=[mg_reg[c][tt] for c in range(NDC)] + [sreg], writes=[bank_reg[b]])
            P.emit("dve", (lambda v, b=b, co=co, ts=ts: v.tensor_tensor(out=h[:, co, ts], in0=bank[b][:, :], in1=h[:, co, ts], op=ALU.add)),
                   writes=[bank_reg[b], h_reg[co][tt]])
    if stop_after == "m4":
        out_h_raw()
        return done()

    P.barrier()
    ffn(1, 2)
    P.barrier()
    of_reg = [Reg("of0"), Reg("of1")]
    for tt in range(NTT):
        ts = slice(tt * 512, (tt + 1) * 512)
        o = of32[tt % 2]
        rmsnorm(3, tt, (lambda dc, o=o: o[:, dc, :]), (lambda dc, tt=tt: [of_reg[tt % 2]]), 6 + tt % 2)
        P.dma("sp", (lambda q, o=o, ts=ts: q.dma_start(out=outT[:, :, ts], in_=o[:, :, :])), "st",
              reads=[of_reg[tt % 2]])
    P.wait_all("sp", ["st"])
    return done()


def replay(nc, P):
    with ExitStack() as es:
        sems = {name: es.enter_context(nc.semaphore(name)) for name in sorted(P.sem_names)}
        block = es.enter_context(nc.Block())

        def run(handle, lst):
            for item in lst:
                if item[0] == "wait":
                    handle.wait_ge(sems[item[1]], item[2])
                else:
                    ins = item[1](handle)
                    if item[2] is not None:
                        ins.then_inc(sems[item[2]], item[3])

        @block.tensor
        def _(t):
            run(t, P.lists["pe"])

        @block.scalar
        def _(a):
            run(a, P.lists["act"])

        @block.vector
        def _(v):
            run(v, P.lists["dve"])

        @block.gpsimd
        def _(g):
            run(g, P.lists["pool"])

        @block.sync
        def _(q):
            run(q, P.lists["sp"])


def _prep_shared(inp):
    f = lambda a: np.ascontiguousarray(np.asarray(a, dtype=np.float32))
    out = {}

    def gu(W):
        return W.reshape(NDC, 128, NFC, 128).transpose(2, 1, 0, 3)

    for n, pre in ((1, "ffn1"), (2, "ffn2")):
        wg = f(inp[pre + "_w_gate"])[0]
        wu = f(inp[pre + "_w_up"])[0]
        wdn = f(inp[pre + "_w_down"])[0]
        out["wgu%d" % n] = f(np.stack([gu(wg), gu(wu)], axis=2).reshape(NFC, 128, 2048))
        out["wd%d" % n] = f(wdn.reshape(2, FH, 128, NDC, 128).transpose(0, 3, 2, 1, 4).reshape(16, 128, FH * 128))
    win = f(inp["w_in"])[0]
    out["wv"] = f(win[:, 1024:1536].reshape(2, 4, 128, 512).transpose(0, 2, 1, 3).reshape(2, 128, 2048))
    out["wz"] = f(win[:, 1536:2048].reshape(NDC, 128, 4, 128).transpose(2, 1, 0, 3).reshape(4, 128, 1024))
    out["wgrp"] = f(inp["pool_w_group"])[0]
    out["wqk"] = f(win[:, 0:1024].reshape(NDC, 128, 2, H, HD).transpose(3, 1, 2, 0, 4).reshape(H, 128, 1024))
    out["wgate"] = f(win[:, 2048:4096].reshape(NDC, 128, 2, 8, 128).transpose(3, 1, 2, 0, 4).reshape(8, 128, 2048))
    wab = np.stack([f(inp["w_branch_attn"])[0], f(inp["w_branch_pool"])[0]], axis=0)
    out["wbr"] = f(wab.reshape(2, 4, 128, 8, 128).transpose(3, 2, 0, 1, 4).reshape(8, 128, 1024))
    out["wout"] = f(f(inp["w_out"])[0].reshape(NDC, 128, 8, 128).transpose(2, 1, 0, 3).reshape(8, 128, 1024))
    g = np.stack([f(inp["ffn1_norm"])[0], f(inp["mix_norm"])[0], f(inp["ffn2_norm"])[0], f(inp["final_norm"])], axis=0)
    out["gains"] = f(g.reshape(4, NDC, 128).transpose(2, 0, 1).reshape(128, 4 * NDC))
    out["pscale"] = f(f(inp["pool_scale"])[0].reshape(4, 128).T)
    out["rpb"] = f(inp["rpb_table"])
    out.update(_host_consts())
    return out


_CACHE = {}


def _run(inputs, stop_after="all", ncores=NB):
    if stop_after not in _CACHE:
        _CACHE[stop_after] = build_program(stop_after)
    nc = _CACHE[stop_after]
    shared = _prep_shared(inputs)
    x = np.asarray(inputs["x"], dtype=np.float32)
    in_maps = []
    for b in range(ncores):
        m = dict(shared)
        m["xT"] = np.ascontiguousarray(x[b].reshape(S, NDC, 128).transpose(2, 1, 0))
        in_maps.append(m)
    res = run_bass_kernel_spmd(nc, in_maps, core_ids=list(range(ncores)))
    out = np.zeros((NB, S, D), np.float32)
    for b in range(ncores):
        out[b] = np.asarray(res.results[b]["outT"]).transpose(2, 1, 0).reshape(S, D)
    return out


def kernel(**inputs):
    return _run(inputs, "all")
```

```python
import math
from contextlib import ExitStack

import numpy as np
import ml_dtypes

import concourse.bass as bass
import concourse.mybir as mybir
from concourse.bass_utils import run_bass_kernel_spmd

F32 = mybir.dt.float32
BF16 = mybir.dt.bfloat16
U8 = mybir.dt.uint8
AF = mybir.ActivationFunctionType
ALU = mybir.AluOpType
AX = mybir.AxisListType

D = 1024
S = 2048
NB = 8
DFF = 2816
NFC = DFF // 128
FH = 11
NDC = D // 128
NTT = S // 512
H = 8
HD = 64
BLK = 256
EPS = 1e-6
NEG = -30000.0
RING_SLOTS = 5
SLOT_BYTES = 4096
PF = 3
FW = 1152
TL = 1280


class Reg:
    __slots__ = ("name", "w", "r")

    def __init__(self, name):
        self.name = name
        self.w = None
        self.r = {}


class Prog:
    ENGS = ("pe", "act", "dve", "pool", "sp")

    def __init__(self):
        self.lists = {e: [] for e in self.ENGS}
        self.cnt = {e: 0 for e in self.ENGS}
        self.waited = {e: {} for e in self.ENGS}
        self.same_wait = {"pe": False, "act": True, "dve": True, "pool": True, "sp": False}
        self.psem = {e: "p_" + e for e in self.ENGS}
        self.dcnt = {}
        self.sem_names = set(self.psem.values())

    def _deps(self, reads, writes):
        deps = {}

        def add(t):
            if t is None:
                return
            s, v = t
            if deps.get(s, 0) < v:
                deps[s] = v

        for r in reads:
            add(r.w)
        for w in writes:
            add(w.w)
            for s, v in w.r.items():
                add((s, v))
        return deps

    def _wait(self, e, deps, skip=None):
        for s, v in deps.items():
            if s == skip:
                continue
            if s == self.psem[e] and not self.same_wait[e]:
                continue
            if self.waited[e].get(s, 0) < v:
                self.lists[e].append(("wait", s, v))
                self.waited[e][s] = v

    def _update(self, tok, reads, writes):
        s, v = tok
        for r in reads:
            if r.r.get(s, 0) < v:
                r.r[s] = v
        for w in writes:
            w.w = tok
            w.r = {}

    def emit(self, e, fns, reads=(), writes=()):
        if not isinstance(fns, (list, tuple)):
            fns = [fns]
        self._wait(e, self._deps(reads, writes))
        for f in fns[:-1]:
            self.lists[e].append(("ins", f, None, 0))
        self.cnt[e] += 1
        self.lists[e].append(("ins", fns[-1], self.psem[e], 1))
        tok = (self.psem[e], self.cnt[e])
        self._update(tok, reads, writes)
        return tok

    def dma(self, q, fn, sem, reads=(), writes=()):
        self.sem_names.add(sem)
        self._wait(q, self._deps(reads, writes), skip=sem)
        self.dcnt[sem] = self.dcnt.get(sem, 0) + 16
        self.lists[q].append(("ins", fn, sem, 16))
        tok = (sem, self.dcnt[sem])
        self._update(tok, reads, writes)
        return tok

    def barrier(self, engs=("pe", "act", "dve", "sp"), dma_sems=()):
        for e in engs:
            deps = {}
            for o in engs:
                if o != e and self.cnt[o] > 0:
                    deps[self.psem[o]] = self.cnt[o]
            for s in dma_sems:
                if self.dcnt.get(s, 0) > 0:
                    deps[s] = self.dcnt[s]
            self._wait(e, deps)

    def wait_all(self, e, sems):
        deps = {s: self.dcnt[s] for s in sems if self.dcnt.get(s, 0) > 0}
        self._wait(e, deps)


def _rpb_bucket_np(dist):
    n = np.maximum(dist, 0)
    max_exact = 16
    nf = np.maximum(n, 1).astype(np.float32)
    large = max_exact + (np.log(nf / np.float32(max_exact)) / np.float32(math.log(128 / max_exact))
                         * np.float32(32 - max_exact)).astype(np.int32)
    large = np.minimum(large, 31)
    return np.where(n < max_exact, n, large)


def _host_consts():
    oh = np.zeros((33, FW), np.float32)
    for i in range(FW):
        d = i - 511
        if d < 0 or i >= 1151:
            oh[32, i] = 1.0
        else:
            oh[int(_rpb_bucket_np(np.array([d], np.int32))[0]), i] = 1.0
    kblk = np.zeros((8, S), np.float32)
    for j in range(8):
        kblk[j, j * BLK:(j + 1) * BLK] = 1.0
    ssum = np.zeros((64, 4, 72), np.float32)
    for ib in range(4, 8):
        for j in range(ib):
            for jp in range(ib):
                ssum[j * 8 + jp, ib - 4, 64 + j] = 1.0
    rc = np.zeros((128, 16), np.float32)
    rc[:, :] = 1.0 / np.arange(1, 17, dtype=np.float32)[None, :]
    return {
        "c_onehot": oh,
        "c_kblk": kblk.astype(ml_dtypes.bfloat16),
        "c_ssum": ssum.reshape(64, 4 * 72).astype(ml_dtypes.bfloat16),
        "c_rc": rc,
    }


def build_program(stop_after="all"):
    nc = bass.Bass("TRN2", target_bir_lowering=False)
    P = Prog()

    def din(name, shape, dt=F32):
        return nc.dram_tensor(name, list(shape), dt, kind="ExternalInput").ap()

    xT = din("xT", [128, NDC, S])
    gains = din("gains", [128, 4 * NDC])
    pscale_d = din("pscale", [128, 4])
    rpb_d = din("rpb", [32, H])
    wgu_d = [din("wgu%d" % n, [NFC, 128, 2048]) for n in (1, 2)]
    wd_d = [din("wd%d" % n, [16, 128, FH * 128]) for n in (1, 2)]
    wv_d = din("wv", [2, 128, 2048])
    wz_d = din("wz", [4, 128, 1024])
    wgrp_d = din("wgrp", [4, 128, 128])
    wqk_d = din("wqk", [H, 128, 1024])
    wgate_d = din("wgate", [8, 128, 2048])
    wbr_d = din("wbr", [8, 128, 1024])
    wout_d = din("wout", [8, 128, 1024])
    c_onehot = din("c_onehot", [33, FW])
    c_kblk = din("c_kblk", [8, S], BF16)
    c_ssum = din("c_ssum", [64, 4 * 72], BF16)
    c_rc = din("c_rc", [128, 16])
    outT = nc.dram_tensor("outT", [128, NDC, S], F32, kind="ExternalOutput").ap()
    fd_t = nc.dram_tensor("fd_scr", [H, FW], BF16, kind="Internal")
    tsk_t = nc.dram_tensor("tsk_scr", [H, 128 * TL], BF16, kind="Internal")

    R1_BYTES = 49152
    SCR_BYTES = 16384
    off = {}
    cur = [0]

    def region(name, nbytes):
        off[name] = cur[0]
        cur[0] += (nbytes + 31) // 32 * 32

    region("h", 65536)
    region("u", 32768)
    region("r1", R1_BYTES)
    region("aT", 16384)
    region("ring", RING_SLOTS * SLOT_BYTES)
    region("rt", 2048)
    region("rstd", 2048)
    region("sq", 3 * 1024)
    region("scr", SCR_BYTES)
    region("gains", 128)
    region("pscale", 32)
    region("onesD", 256)
    region("ones64", 128)
    region("cbias", 32)
    region("rc", 64)
    region("tab33", 32)
    region("ssum", 576)
    TOTAL = cur[0]
    big = nc.alloc_sbuf_tensor("big", [128, TOTAL], U8)

    def view(base, nbytes, dt, pattern=None, **kw):
        v = big[:, base:base + nbytes].bitcast(dt)
        if pattern:
            v = v.rearrange(pattern, **kw)
        return v

    h = view(off["h"], 65536, F32, "p (a t) -> p a t", a=NDC)
    u = view(off["u"], 32768, BF16, "p (a t) -> p a t", a=NDC)
    r1 = off["r1"]
    A = view(r1, FH * 4096, BF16, "p (a t) -> p a t", a=FH)
    pT = view(r1, 16384, BF16, "p (a t) -> p a t", a=4)
    Va = view(r1 + 16384, 16640, BF16, "p (i c) -> p i c", i=16)
    QaB = [view(r1 + 33024, 4096, BF16), view(r1, 4096, BF16)]
    KaB = [view(r1 + 37120, 4096, BF16), view(r1 + 4096, 4096, BF16)]
    G2 = [view(r1 + 41216 + i * 2048, 2048, BF16) for i in range(2)]
    merged = view(r1 + 16384, 32768, BF16, "p (a t) -> p a t", a=NDC)
    of32 = [view(r1 + i * 16384, 16384, F32, "p (a t) -> p a t", a=NDC) for i in range(2)]
    aT = view(off["aT"], 16384, BF16, "p (a t) -> p a t", a=4)
    zf = view(r1 + 16384, 8192, F32)
    ta = view(r1 + 24576, 8192, F32)
    tb = view(r1 + 32768, 8192, F32)
    ta_bf = view(r1 + 24576, 4096, BF16)
    tb_bf = view(r1 + 32768, 4096, BF16)
    ring = [view(off["ring"] + i * SLOT_BYTES, SLOT_BYTES, BF16) for i in range(RING_SLOTS)]
    rt = view(off["rt"], 2048, F32)
    rstd = view(off["rstd"], 2048, F32)
    sq = [view(off["sq"] + i * 1024, 1024, BF16) for i in range(3)]
    scr = off["scr"]
    gains_sb = view(off["gains"], 128, F32)
    pscale_sb = view(off["pscale"], 16, F32)
    onesD = view(off["onesD"], 256, BF16)
    ones64 = view(off["ones64"], 128, BF16)
    cbias = view(off["cbias"], 32, F32)
    rc_sb = view(off["rc"], 64, F32)
    tab33 = view(off["tab33"], 32, F32)
    ssum_sb = view(off["ssum"], 576, BF16, "p (a c) -> p a c", a=4)
    sg = [view(scr + i * 1024, 1024, BF16) for i in range(3)]
    onehot_sb = view(off["aT"], FW * 4, F32)
    fexp_sb = view(off["aT"] + FW * 4, FW * 2, BF16)
    Pt = [view(scr + i * 1024, 1024, BF16) for i in range(6)]
    Et = [view(scr + 6144 + i * 1024, 1024, BF16) for i in range(2)]
    dhi = [view(scr + 8192 + i * 1024, 1024, BF16) for i in range(2)]
    dlo = [view(scr + 10240 + i * 1024, 1024, BF16) for i in range(2)]
    rec = [view(scr + 12288, 2048, F32)]
    stg = [view(scr + 14336 + i * 1024, 1024, BF16) for i in range(2)]
    ind = [view(r1 + 45312 + i * 1024, 1024, BF16) for i in range(2)]
    kmT = view(r1 + 47360, 32, F32)
    diffT = view(r1 + 47392, 128, BF16)
    tmp16 = view(scr, 64, F32)
    sgm = [view(scr + i * 2048, 2048, F32) for i in range(2)]
    m1 = [view(scr + 4096 + i * 2048, 2048, F32) for i in range(2)]

    bank = [nc.alloc_psum_tensor("bank%d" % i, [128, 512], F32) for i in range(8)]
    bank_reg = [Reg("bank%d" % i) for i in range(8)]

    h_reg = [[Reg("h%d_%d" % (dc, tt)) for tt in range(NTT)] for dc in range(NDC)]
    u_reg = [Reg("u%d" % tt) for tt in range(NTT)]
    A_reg = [[Reg("A%d_%d" % (f, tt)) for tt in range(NTT)] for f in range(FH)]
    sq_reg = [Reg("sq%d" % i) for i in range(3)]
    rt_reg = Reg("rt")
    rstd_reg = Reg("rstd")
    sg_reg = [Reg("sg%d" % i) for i in range(3)]
    slot_reg = [Reg("slot%d" % i) for i in range(RING_SLOTS)]
    const_reg = Reg("consts")
    G_reg = Reg("G")

    slabs = []

    def ffn_slabs(n):
        for fh in range(2):
            for fcl in range(FH):
                slabs.append([(wgu_d[n][fh * FH + fcl], 0, 2048)])
            for dco in range(NDC):
                slabs.append([(wd_d[n][fh * NDC + dco], 0, FH * 128)])

    ffn_slabs(0)
    slabs.append([(wv_d[0], 0, 2048)])
    slabs.append([(wv_d[1], 0, 2048)])
    for hh in range(H):
        slabs.append([(wqk_d[hh], 0, 1024)])
    for g in range(4):
        slabs.append([(wz_d[g], 0, 1024), (wgrp_d[g], 1024, 128)])
    for c in range(8):
        slabs.append([(wgate_d[c], 0, 2048)])
        slabs.append([(wbr_d[c], 0, 1024)])
    for c in range(8):
        slabs.append([(wout_d[c], 0, 1024)])
    ffn_slabs(1)
    state = {"issued": 0, "next": 0}

    def issue_slabs(upto):
        while state["issued"] < min(upto, len(slabs)):
            i = state["issued"]
            s = i % RING_SLOTS
            for (src, o, n) in slabs[i]:
                dst = ring[s][:, o:o + n]
                P.dma("pool", (lambda g, dst=dst, src=src: g.dma_start(out=dst, in_=src)),
                      "ring%d" % s, writes=[slot_reg[s]])
            state["issued"] += 1

    def next_slab():
        i = state["next"]
        state["next"] += 1
        issue_slabs(i + PF + 1)
        s = i % RING_SLOTS
        return ring[s], slot_reg[s]

    def mm(out, lhsT, rhs, start, stop):
        return lambda t: t.matmul(out, lhsT, rhs, start=start, stop=stop)

    sqi = [0]

    def rmsnorm(n, tt, dst_fn, dst_regs, ssq_bank):
        ts = slice(tt * 512, (tt + 1) * 512)
        for dc in range(NDC):
            k = sqi[0] % 3
            sqi[0] += 1
            P.emit("act", (lambda a, k=k, dc=dc: a.activation(out=sq[k][:, :], in_=h[:, dc, ts], func=AF.Square)),
                   reads=[h_reg[dc][tt]], writes=[sq_reg[k]])
            P.emit("pe", mm(bank[ssq_bank][:, :], onesD[:, :], sq[k][:, :], dc == 0, dc == NDC - 1),
                   reads=[sq_reg[k]], writes=[bank_reg[ssq_bank]])
        P.emit("act", (lambda a: a.activation(out=rt[:, :], in_=bank[ssq_bank][:, :], func=AF.Sqrt, bias=EPS, scale=1.0)),
               writes=[bank_reg[ssq_bank], rt_reg])
        P.emit("dve", (lambda v: v.reciprocal(out=rstd[:, :], in_=rt[:, :])), reads=[rt_reg], writes=[rstd_reg])
        for dc in range(NDC):
            gcol = gains_sb[:, n * NDC + dc:n * NDC + dc + 1]
            P.emit("dve", (lambda v, dc=dc, gcol=gcol: v.scalar_tensor_tensor(
                out=dst_fn(dc), in0=h[:, dc, ts], scalar=gcol, in1=rstd[:, :], op0=ALU.mult, op1=ALU.mult)),
                reads=[h_reg[dc][tt], rstd_reg, const_reg], writes=dst_regs(dc))

    def ffn(n, gain_idx):
        for tt in range(NTT):
            ts = slice(tt * 512, (tt + 1) * 512)
            rmsnorm(gain_idx, tt, (lambda dc, ts=ts: u[:, dc, ts]), (lambda dc, tt=tt: [u_reg[tt]]), 6 + tt % 2)
        k = 0
        kk = 0
        for fh in range(2):
            for fcl in range(FH):
                slab, sreg = next_slab()
                wg = slab[:, 0:1024].rearrange("p (a c) -> p a c", a=NDC)
                wu = slab[:, 1024:2048].rearrange("p (a c) -> p a c", a=NDC)
                for tt in range(NTT):
                    ts = slice(tt * 512, (tt + 1) * 512)
                    gb = k % 2
                    ub = 2 + k % 2
                    si = k % 3
                    k += 1
                    P.emit("pe", [mm(bank[gb][:, :], wg[:, dc, :], u[:, dc, ts], dc == 0, dc == NDC - 1) for dc in range(NDC)],
                           reads=[u_reg[tt], sreg], writes=[bank_reg[gb]])
                    P.emit("pe", [mm(bank[ub][:, :], wu[:, dc, :], u[:, dc, ts], dc == 0, dc == NDC - 1) for dc in range(NDC)],
                           reads=[u_reg[tt], sreg], writes=[bank_reg[ub]])
                    P.emit("act", (lambda a, gb=gb, si=si: a.activation(out=sg[si][:, :], in_=bank[gb][:, :], func=AF.Silu)),
                           writes=[bank_reg[gb], sg_reg[si]])
                    P.emit("dve", (lambda v, ub=ub, si=si, fcl=fcl, ts=ts: v.tensor_tensor(
                        out=A[:, fcl, ts], in0=bank[ub][:, :], in1=sg[si][:, :], op=ALU.mult)),
                        reads=[sg_reg[si]], writes=[bank_reg[ub], A_reg[fcl][tt]])
            for dco in range(NDC):
                slab, sreg = next_slab()
                wdv = slab[:, 0:FH * 128].rearrange("p (a c) -> p a c", a=FH)
                for tt in range(NTT):
                    ts = slice(tt * 512, (tt + 1) * 512)
                    yb = 4 + kk % 2
                    kk += 1
                    P.emit("pe", [mm(bank[yb][:, :], wdv[:, f, :], A[:, f, ts], f == 0, f == FH - 1) for f in range(FH)],
                           reads=[A_reg[f][tt] for f in range(FH)] + [sreg], writes=[bank_reg[yb]])
                    P.emit("dve", (lambda v, yb=yb, dco=dco, ts=ts: v.scalar_tensor_tensor(
                        out=h[:, dco, ts], in0=bank[yb][:, :], scalar=0.5, in1=h[:, dco, ts], op0=ALU.mult, op1=ALU.add)),
                        writes=[bank_reg[yb], h_reg[dco][tt]])

    for tt in range(NTT):
        ts = slice(tt * 512, (tt + 1) * 512)
        P.dma("sp", (lambda q, ts=ts: q.dma_start(out=h[:, :, ts], in_=xT[:, :, ts])), "x%d" % tt,
              writes=[h_reg[dc][tt] for dc in range(NDC)])
    for (dst, src) in (
        (gains_sb[:, :], gains),
        (pscale_sb[:, :], pscale_d),
        (rc_sb[:, :], c_rc),
        (ssum_sb[0:64, :, :], c_ssum.rearrange("p (a c) -> p a c", a=4)),
        (tab33[0:32, :], rpb_d),
        (cbias[:, :], rpb_d[31:32, :].partition_broadcast(128)),
        (onehot_sb[0:33, :], c_onehot),
    ):
        P.dma("sp", (lambda q, dst=dst, src=src: q.dma_start(out=dst, in_=src)), "cst", writes=[const_reg])
    P.emit("dve", (lambda v: v.memset(onesD[:, :], 1.0 / D)), writes=[const_reg])
    P.emit("dve", (lambda v: v.memset(ones64[:, :], 1.0)), writes=[const_reg])
    P.emit("dve", (lambda v: v.memset(tab33[32:33, :], NEG)), writes=[const_reg])
    for e in ("pe", "act", "dve"):
        P.wait_all(e, ["cst"])
    P.barrier(engs=("pe", "act", "dve"))

    fb = 0
    for (c0, c1) in ((0, 512), (512, 1024), (1024, FW)):
        P.emit("pe", mm(bank[fb][0:8, 0:c1 - c0], tab33[0:33, 0:8], onehot_sb[0:33, c0:c1], True, True), writes=[bank_reg[fb]])
        P.emit("act", (lambda a, c0=c0, c1=c1: a.activation(out=fexp_sb[0:8, c0:c1], in_=bank[fb][0:8, 0:c1 - c0], func=AF.Exp)),
               writes=[bank_reg[fb], G_reg])
    P.dma("sp", (lambda q: q.dma_start(out=fd_t.ap(), in_=fexp_sb[0:8, :])), "g1", reads=[G_reg], writes=[G_reg])
    for hh in range(H):
        src = bass.AP(fd_t, hh * FW, [[0, 128], [1, 1151]])
        dst = bass.AP(tsk_t, hh * 128 * TL, [[TL + 1, 128], [1, 1151]])
        P.dma("sp", (lambda q, dst=dst, src=src: q.dma_start(out=dst, in_=src)), "g2", reads=[G_reg], writes=[])
    G2_reg = Reg("G2")
    G2_reg.w = ("g2", P.dcnt["g2"])
    Gh_reg = [Reg("Gh0"), Reg("Gh1")]

    def load_G(hh):
        src = bass.AP(tsk_t, hh * 128 * TL + 127, [[TL, 128], [1, 1024]])
        P.dma("sp", (lambda q: q.dma_start(out=G2[hh % 2][:, :], in_=src)), "gh%d" % (hh % 2),
              reads=[G2_reg], writes=[Gh_reg[hh % 2]])

    def out_h_raw():
        for tt in range(NTT):
            ts = slice(tt * 512, (tt + 1) * 512)
            P.dma("sp", (lambda q, ts=ts: q.dma_start(out=outT[:, :, ts], in_=h[:, :, ts])), "st",
                  reads=[h_reg[dc][tt] for dc in range(NDC)])
        P.wait_all("sp", ["st"])

    def done():
        replay(nc, P)
        return nc

    ffn(0, 0)
    if stop_after == "ffn1":
        out_h_raw()
        return done()

    P.barrier()
    for tt in range(NTT):
        ts = slice(tt * 512, (tt + 1) * 512)
        rmsnorm(1, tt, (lambda dc, ts=ts: u[:, dc, ts]), (lambda dc, tt=tt: [u_reg[tt]]), 6 + tt % 2)

    Va4 = Va.rearrange("p i (h c) -> p i h c", c=65)
    Va_reg = [Reg("Va%d" % i) for i in range(16)]
    s0, sr0 = next_slab()
    s1, sr1 = next_slab()
    wv = [s0.rearrange("p (a c) -> p a c", a=4), s1.rearrange("p (a c) -> p a c", a=4)]
    P.emit("dve", (lambda v: v.memset(Va4[:, :, :, 64:65], 1.0)), writes=Va_reg)
    for i in range(16):
        b = i % 4
        tsl = slice(i * 128, (i + 1) * 128)
        P.emit("pe", [mm(bank[b][:, :], u[:, dc, tsl], wv[dc // 4][:, dc % 4, :], dc == 0, dc == NDC - 1) for dc in range(NDC)],
               reads=[u_reg[i // 4], sr0, sr1], writes=[bank_reg[b]])
        src = bank[b][:, :].rearrange("p (h c) -> p h c", c=64)
        if i % 2 == 0:
            P.emit("act", (lambda a, i=i, src=src: a.activation(out=Va4[:, i, :, 0:64], in_=src, func=AF.Copy)),
                   writes=[bank_reg[b], Va_reg[i]])
        else:
            P.emit("dve", (lambda v, i=i, src=src: v.tensor_copy(out=Va4[:, i, :, 0:64], in_=src)),
                   writes=[bank_reg[b], Va_reg[i]])

    Kb_reg = [Reg("Kb0"), Reg("Kb1")]
    Qz_reg = [Reg("Qz0"), Reg("Qz1")]
    Qm_reg = [[Reg("Qm%d_%d" % (b, ib)) for ib in range(8)] for b in range(2)]
    Qa_reg = [[Reg("Qa%d_%d" % (b, tt)) for tt in range(NTT)] for b in range(2)]
    Ka_reg = [[Reg("Ka%d_%d" % (b, tt)) for tt in range(NTT)] for b in range(2)]
    km_reg = Reg("km")
    df_reg = Reg("df")
    ind_reg = [Reg("ind0"), Reg("ind1")]
    Pt_reg = [Reg("Pt%d" % i) for i in range(6)]
    Et_reg = [Reg("Et%d" % i) for i in range(2)]
    dhi_reg = [Reg("dhi%d" % i) for i in range(2)]
    dlo_reg = [Reg("dlo%d" % i) for i in range(2)]
    rec_reg = [Reg("rec0")]
    stg_reg = [Reg("stg%d" % i) for i in range(2)]
    aT_reg = [[Reg("aT%d_%d" % (hh, m)) for m in range(NTT)] for hh in range(H)]
    for b in range(2):
        P.dma("sp", (lambda q, b=b: q.dma_start(out=KaB[b][64:72, :], in_=c_kblk)), "cst2", writes=[Kb_reg[b]])
        P.emit("dve", (lambda v, b=b: v.memset(QaB[b][64:72, :], 0.0)), writes=[Qz_reg[b]] + Qm_reg[b])
    cnts = {"s": 0, "p": 0, "e": 0, "f": 0, "g": 0, "o": 0}
    SCALE = HD ** -0.5
    load_G(0)

    pj = {"k": 0}

    def proj_groups(hh):
        b = hh % 2
        slab, sreg = next_slab()
        wq = slab[:, 0:512].rearrange("p (a c) -> p a c", a=NDC)
        wk = slab[:, 512:1024].rearrange("p (a c) -> p a c", a=NDC)
        out = []
        for tt in range(NTT):
            ts = slice(tt * 512, (tt + 1) * 512)
            for (wm, dstb, dreg) in ((wq, QaB[b], Qa_reg[b][tt]), (wk, KaB[b], Ka_reg[b][tt])):
                pb = (0, 2)[pj["k"] % 2]
                pj["k"] += 1
                for dc in range(NDC):
                    def one(wm=wm, ts=ts, tt=tt, dc=dc, pb=pb):
                        P.emit("pe", mm(bank[pb][0:64, :], wm[:, dc, :], u[:, dc, ts], dc == 0, dc == NDC - 1),
                               reads=[u_reg[tt], sreg], writes=[bank_reg[pb]])
                    out.append(one)

                def evac(dstb=dstb, dreg=dreg, ts=ts, pb=pb):
                    P.emit("dve", (lambda v: v.tensor_copy(out=dstb[0:64, ts], in_=bank[pb][0:64, :])),
                           writes=[bank_reg[pb], dreg])
                out.append(evac)
        return out

    def attn_head(hh, nxt):
        b = hh % 2
        Qa, Ka = QaB[b], KaB[b]
        if hh + 1 < H:
            load_G(hh + 1)
        Gt = G2[hh % 2]
        Greg = Gh_reg[hh % 2]
        P.emit("dve", (lambda v: v.reduce_sum(out=kmT[0:64, 0:8], in_=Ka[0:64, :].rearrange("p (j k) -> p j k", k=BLK), axis=AX.X)),
               reads=Ka_reg[b], writes=[km_reg])
        P.emit("dve", (lambda v: v.tensor_tensor(
            out=diffT[0:64, :].rearrange("p (a b) -> p a b", b=8),
            in0=kmT[0:64, 0:8].unsqueeze(1).broadcast_to([64, 8, 8]),
            in1=kmT[0:64, 0:8].unsqueeze(2).broadcast_to([64, 8, 8]), op=ALU.subtract)),
            reads=[km_reg], writes=[df_reg])

        def selection():
            for half in range(2):
                qs = slice(1024 + half * 512, 1536 + half * 512)
                P.emit("pe", mm(bank[1][0:64, :], diffT[0:64, 0:64], Qa[0:64, qs], True, True),
                       reads=[df_reg, Qa_reg[b][2 + half]], writes=[bank_reg[1]])
                P.emit("dve", (lambda v, half=half: v.tensor_single_scalar(
                    out=ind[half][0:64, :], in_=bank[1][0:64, :], scalar=0.0, op=ALU.is_gt)),
                    writes=[bank_reg[1], ind_reg[half]])
                for blk in range(2):
                    ib = 4 + half * 2 + blk
                    P.emit("pe", mm(bank[1][0:72, 0:256], ssum_sb[0:64, ib - 4, :], ind[half][0:64, blk * 256:(blk + 1) * 256], True, True),
                           reads=[ind_reg[half]], writes=[bank_reg[1]])
                    P.emit("dve", (lambda v, ib=ib: v.tensor_scalar(
                        out=Qa[64:72, ib * BLK:(ib + 1) * BLK], in0=bank[1][64:72, 0:256],
                        scalar1=2.5, scalar2=NEG, op0=ALU.is_gt, op1=ALU.mult)),
                        writes=[bank_reg[1], Qm_reg[b][ib]])

        tiles = [(m, kt) for m in range(NTT) for kt in range(4 * m + 4)]
        info = {}
        pending = []
        DEPTH = 3
        DELAY = 4
        SRING = (3, 4, 5)

        def s_stage(idx):
            m, kt = tiles[idx]
            q0 = m * 512
            sb = SRING[cnts["s"] % len(SRING)]
            cnts["s"] += 1
            pi = cnts["p"] % 6
            cnts["p"] += 1
            info[idx] = pi
            delta = kt * 128 - q0
            cs = max(0, delta)
            rd = [Ka_reg[b][kt // 4], Kb_reg[b], Qa_reg[b][m]] + ([Qz_reg[b]] if m < 2 else [Qm_reg[b][2 * m], Qm_reg[b][2 * m + 1]])
            P.emit("pe", mm(bank[sb][:, cs:512], Ka[0:72, kt * 128:(kt + 1) * 128], Qa[0:72, q0 + cs:q0 + 512], True, True),
                   reads=rd, writes=[bank_reg[sb]])
            if delta < -128:
                P.emit("act", (lambda a, sb=sb, pi=pi: a.activation(
                    out=Pt[pi][:, :], in_=bank[sb][:, :], func=AF.Exp, bias=cbias[:, hh:hh + 1], scale=SCALE)),
                    writes=[bank_reg[sb], Pt_reg[pi]])
            else:
                ei = cnts["e"] % 2
                cnts["e"] += 1
                goff = 384 - delta
                P.emit("act", (lambda a, sb=sb, ei=ei, cs=cs: a.activation(
                    out=Et[ei][:, cs:512], in_=bank[sb][:, cs:512], func=AF.Exp, scale=SCALE)),
                    writes=[bank_reg[sb], Et_reg[ei]])
                eng = "pool" if cnts["o"] % 2 == 1 else "dve"
                cnts["o"] += 1
                P.emit(eng, (lambda g, ei=ei, pi=pi, goff=goff, cs=cs: g.tensor_tensor(
                    out=Pt[pi][:, cs:512], in0=Et[ei][:, cs:512], in1=Gt[:, goff + cs:goff + 512], op=ALU.mult)),
                    reads=[Et_reg[ei], Greg], writes=[Pt_reg[pi]])

        def pv_stage(idx):
            m, kt = tiles[idx]
            q0 = m * 512
            pi = info[idx]
            ob = 6 + m % 2
            last = 4 * m + 3
            cs = max(0, kt * 128 - q0)
            P.emit("pe", mm(bank[ob][0:65, cs:512], Va4[:, kt, hh, :], Pt[pi][:, cs:512], kt == 0, kt == last),
                   reads=[Va_reg[kt], Pt_reg[pi]], writes=[bank_reg[ob]])
            if kt != last:
                return
            fi = cnts["f"] % 2
            cnts["f"] += 1
            P.emit("dve", (lambda v, ob=ob, fi=fi: v.tensor_copy(out=dhi[fi][64:65, :], in_=bank[ob][64:65, :])),
                   writes=[bank_reg[ob], dhi_reg[fi]])
            P.emit("dve", (lambda v, ob=ob, fi=fi: v.tensor_tensor(
                out=dlo[fi][64:65, :], in0=bank[ob][64:65, :], in1=dhi[fi][64:65, :], op=ALU.subtract)),
                reads=[dhi_reg[fi]], writes=[bank_reg[ob], dlo_reg[fi]])

            def stage_b(ob=ob, fi=fi, m=m, q0=q0):
                P.emit("pe", [mm(bank[1][0:64, :], ones64[64:65, 0:64], dhi[fi][64:65, :], True, False),
                              mm(bank[1][0:64, :], ones64[64:65, 0:64], dlo[fi][64:65, :], False, True)],
                       reads=[dhi_reg[fi], dlo_reg[fi]], writes=[bank_reg[1]])
                P.emit("dve", (lambda v: v.reciprocal(out=rec[0][0:64, :], in_=bank[1][0:64, :])),
                       writes=[bank_reg[1], rec_reg[0]])
                c = hh // 2
                if hh % 2 == 0:
                    P.emit("dve", (lambda v: v.tensor_tensor(
                        out=aT[0:64, c, q0:q0 + 512], in0=bank[ob][0:64, :], in1=rec[0][0:64, :], op=ALU.mult)),
                        reads=[rec_reg[0]], writes=[bank_reg[ob], aT_reg[hh][m]])
                else:
                    gi = cnts["g"] % 2
                    cnts["g"] += 1
                    P.emit("dve", (lambda v: v.tensor_tensor(
                        out=stg[gi][0:64, :], in0=bank[ob][0:64, :], in1=rec[0][0:64, :], op=ALU.mult)),
                        reads=[rec_reg[0]], writes=[bank_reg[ob], stg_reg[gi]])
                    P.dma("sp", (lambda q: q.dma_start(out=aT[64:128, c, q0:q0 + 512], in_=stg[gi][0:64, :])),
                          "stg%d" % gi, reads=[stg_reg[gi]], writes=[aT_reg[hh][m]])

            pending.append([DELAY, stage_b])

        n = len(tiles)
        for idx in range(n + DEPTH):
            if idx == 4:
                selection()
            if idx < n:
                s_stage(idx)
            if idx - DEPTH >= 0:
                pv_stage(idx - DEPTH)
            for _ in range(2):
                if nxt and idx >= 1:
                    nxt.pop(0)()
            for pb in list(pending):
                pb[0] -= 1
                if pb[0] <= 0:
                    pending.remove(pb)
                    pb[1]()
        for pb in pending:
            pb[1]()
        del pending[:]
        while nxt:
            nxt.pop(0)()

    cur = proj_groups(0)
    for g in cur:
        g()
    for hh in range(H):
        nxt = proj_groups(hh + 1) if hh + 1 < H else []
        attn_head(hh, nxt)
    if stop_after == "m3":
        for c in range(4):
            for tt in range(NTT):
                ts = slice(tt * 512, (tt + 1) * 512)
                P.emit("act", (lambda a, c=c, ts=ts: a.activation(out=h[:, c, ts], in_=aT[:, c, ts], func=AF.Copy)),
                       reads=[aT_reg[hh][tt] for hh in (2 * c, 2 * c + 1)],
                       writes=[h_reg[c][tt]])
        out_h_raw()
        return done()

    P.barrier(dma_sems=("stg0", "stg1"))
    zf_reg = [Reg("zf%d" % tt) for tt in range(NTT)]
    ta_reg = Reg("ta")
    tb_reg = Reg("tb")
    t16_reg = Reg("t16")
    pT_reg = [[Reg("pT%d_%d" % (g, tt)) for tt in range(NTT)] for g in range(4)]
    zslab = {}

    def z_proj(g):
        slab, sreg = next_slab()
        zslab[g] = (slab, sreg)
        wzv = slab[:, 0:1024].rearrange("p (a c) -> p a c", a=NDC)
        for tt in range(NTT):
            ts = slice(tt * 512, (tt + 1) * 512)
            b = (g % 2) * 4 + tt
            P.emit("pe", [mm(bank[b][:, :], wzv[:, dc, :], u[:, dc, ts], dc == 0, dc == NDC - 1) for dc in range(NDC)],
                   reads=[u_reg[tt], sreg], writes=[bank_reg[b]])

    z_proj(0)
    for g in range(4):
        w = 2 ** (g + 1)
        if g < 3:
            z_proj(g + 1)
        slab, sreg = zslab[g]
        wgg = slab[:, 1024:1152]
        for tt in range(NTT):
            ts = slice(tt * 512, (tt + 1) * 512)
            b = (g % 2) * 4 + tt
            P.emit("act", (lambda a, b=b, ts=ts: a.activation(out=zf[:, ts], in_=bank[b][:, :], func=AF.Copy)),
                   writes=[bank_reg[b], zf_reg[tt]])
        bufs = [(ta, ta_reg, ta_bf), (tb, tb_reg, tb_bf)]
        src, src_regs = zf, list(zf_reg)
        sft = 1
        idx = 0
        while sft < w:
            dst, dreg, _ = bufs[idx % 2]
            P.emit("dve", (lambda v, dst=dst, src=src, sft=sft: v.tensor_tensor(
                out=dst[:, sft:S], in0=src[:, sft:S], in1=src[:, 0:S - sft], op=ALU.add)),
                reads=src_regs, writes=[dreg])
            P.emit("dve", (lambda v, dst=dst, src=src, sft=sft: v.tensor_copy(out=dst[:, 0:sft], in_=src[:, 0:sft])),
                   reads=src_regs, writes=[dreg])
            src, src_regs = dst, [dreg]
            idx += 1
            sft *= 2
        fin, fin_regs = src, src_regs
        _, oreg, pooled = bufs[idx % 2]
        P.emit("dve", (lambda v, fin=fin, pooled=pooled, w=w: v.scalar_tensor_tensor(
            out=pooled[:, :], in0=fin[:, :], scalar=1.0 / w, in1=zf[:, :], op0=ALU.mult, op1=ALU.subtract)),
            reads=fin_regs + zf_reg, writes=[oreg])
        P.emit("dve", (lambda v, fin=fin, w=w: v.tensor_tensor(
            out=tmp16[:, 0:w - 1], in0=fin[:, 0:w - 1], in1=rc_sb[:, 0:w - 1], op=ALU.mult)),
            reads=fin_regs, writes=[t16_reg])
        P.emit("dve", (lambda v, pooled=pooled, w=w: v.tensor_tensor(
            out=pooled[:, 0:w - 1], in0=tmp16[:, 0:w - 1], in1=zf[:, 0:w - 1], op=ALU.subtract)),
            reads=[t16_reg] + zf_reg, writes=[oreg])
        for tt in range(NTT):
            ts = slice(tt * 512, (tt + 1) * 512)
            b = (g % 2) * 4 + tt
            P.emit("pe", mm(bank[b][:, :], wgg, pooled[:, ts], True, True), reads=[oreg, sreg], writes=[bank_reg[b]])
            P.emit("dve", (lambda v, b=b, g=g, ts=ts: v.tensor_single_scalar(
                out=pT[:, g, ts], in_=bank[b][:, :], scalar=pscale_sb[:, g:g + 1], op=ALU.mult)),
                writes=[bank_reg[b], pT_reg[g][tt]])

    P.barrier(dma_sems=("stg0", "stg1"))
    mg_reg = [[Reg("mg%d_%d" % (c, tt)) for tt in range(NTT)] for c in range(NDC)]
    sgm_reg = [Reg("sgm0"), Reg("sgm1")]
    m1_reg = [Reg("m10"), Reg("m11")]
    k4 = 0
    for c in range(NDC):
        slabg, sgr = next_slab()
        gaw = slabg[:, 0:1024].rearrange("p (a c) -> p a c", a=NDC)
        gpw = slabg[:, 1024:2048].rearrange("p (a c) -> p a c", a=NDC)
        slabb, sbr = next_slab()
        waw = slabb[:, 0:512].rearrange("p (a c) -> p a c", a=4)
        wpw = slabb[:, 512:1024].rearrange("p (a c) -> p a c", a=4)
        for tt in range(NTT):
            ts = slice(tt * 512, (tt + 1) * 512)
            b0 = 4 * (k4 % 2)
            k4 += 1
            a_rd = [aT_reg[hh][tt] for hh in range(H)]
            P.emit("pe", [mm(bank[b0][:, :], gaw[:, dc, :], u[:, dc, ts], dc == 0, dc == NDC - 1) for dc in range(NDC)],
                   reads=[u_reg[tt], sgr], writes=[bank_reg[b0]])
            P.emit("pe", [mm(bank[b0 + 1][:, :], waw[:, cc, :], aT[:, cc, ts], cc == 0, cc == 3) for cc in range(4)],
                   reads=a_rd + [sbr], writes=[bank_reg[b0 + 1]])
            P.emit("pe", [mm(bank[b0 + 2][:, :], gpw[:, dc, :], u[:, dc, ts], dc == 0, dc == NDC - 1) for dc in range(NDC)],
                   reads=[u_reg[tt], sgr], writes=[bank_reg[b0 + 2]])
            P.emit("pe", [mm(bank[b0 + 3][:, :], wpw[:, g, :], pT[:, g, ts], g == 0, g == 3) for g in range(4)],
                   reads=[pT_reg[g][tt] for g in range(4)] + [sbr], writes=[bank_reg[b0 + 3]])
            P.emit("act", (lambda a, b0=b0: a.activation(out=sgm[0][:, :], in_=bank[b0][:, :], func=AF.Sigmoid)),
                   writes=[bank_reg[b0], sgm_reg[0]])
            P.emit("act", (lambda a, b0=b0: a.activation(out=sgm[1][:, :], in_=bank[b0 + 2][:, :], func=AF.Sigmoid)),
                   writes=[bank_reg[b0 + 2], sgm_reg[1]])
            P.emit("dve", (lambda v, b0=b0: v.tensor_tensor(out=m1[0][:, :], in0=bank[b0 + 1][:, :], in1=sgm[0][:, :], op=ALU.mult)),
                   reads=[sgm_reg[0]], writes=[bank_reg[b0 + 1], m1_reg[0]])
            P.emit("dve", (lambda v, b0=b0: v.tensor_tensor(out=m1[1][:, :], in0=bank[b0 + 3][:, :], in1=sgm[1][:, :], op=ALU.mult)),
                   reads=[sgm_reg[1]], writes=[bank_reg[b0 + 3], m1_reg[1]])
            P.emit("dve", (lambda v, c=c, ts=ts: v.tensor_tensor(out=merged[:, c, ts], in0=m1[0][:, :], in1=m1[1][:, :], op=ALU.add)),
                   reads=[m1_reg[0], m1_reg[1]], writes=[mg_reg[c][tt]])
    ko = 0
    for co in range(NDC):
        slab, sreg = next_slab()
        wo = slab[:, 0:1024].rearrange("p (a c) -> p a c", a=NDC)
        for tt in range(NTT):
            ts = slice(tt * 512, (tt + 1) * 512)
            b = ko % 2
            ko += 1
            P.emit("pe", [mm(bank[b][:, :], wo[:, c, :], merged[:, c, ts], c == 0, c == NDC - 1) for c in range(NDC)],
                   reads=[mg_reg[c][tt] for c in range(NDC)] + [sreg], writes=[bank_reg[b]])
            P.emit("dve", (lambda v, b=b, co=co, ts=ts: v.tensor_tensor(out=h[:, co, ts], in0=bank[b][:, :], in1=h[:, co, ts], op=ALU.add)),
                   writes=[bank_reg[b], h_reg[co][tt]])
    if stop_after == "m4":
        out_h_raw()
        return done()

    P.barrier()
    ffn(1, 2)
    P.barrier()
    of_reg = [Reg("of0"), Reg("of1")]
    for tt in range(NTT):
        ts = slice(tt * 512, (tt + 1) * 512)
        o = of32[tt % 2]
        rmsnorm(3, tt, (lambda dc, o=o: o[:, dc, :]), (lambda dc, tt=tt: [of_reg[tt % 2]]), 6 + tt % 2)
        P.dma("sp", (lambda q, o=o, ts=ts: q.dma_start(out=outT[:, :, ts], in_=o[:, :, :])), "st",
              reads=[of_reg[tt % 2]])
    P.wait_all("sp", ["st"])
    return done()


def replay(nc, P):
    with ExitStack() as es:
        sems = {name: es.enter_context(nc.semaphore(name)) for name in sorted(P.sem_names)}
        block = es.enter_context(nc.Block())

        def run(handle, lst):
            for item in lst:
                if item[0] == "wait":
                    handle.wait_ge(sems[item[1]], item[2])
                else:
                    ins = item[1](handle)
                    if item[2] is not None:
                        ins.then_inc(sems[item[2]], item[3])

        @block.tensor
        def _(t):
            run(t, P.lists["pe"])

        @block.scalar
        def _(a):
            run(a, P.lists["act"])

        @block.vector
        def _(v):
            run(v, P.lists["dve"])

        @block.gpsimd
        def _(g):
            run(g, P.lists["pool"])

        @block.sync
        def _(q):
            run(q, P.lists["sp"])


def _prep_shared(inp):
    f = lambda a: np.ascontiguousarray(np.asarray(a, dtype=np.float32))
    out = {}

    def gu(W):
        return W.reshape(NDC, 128, NFC, 128).transpose(2, 1, 0, 3)

    for n, pre in ((1, "ffn1"), (2, "ffn2")):
        wg = f(inp[pre + "_w_gate"])[0]
        wu = f(inp[pre + "_w_up"])[0]
        wdn = f(inp[pre + "_w_down"])[0]
        out["wgu%d" % n] = f(np.stack([gu(wg), gu(wu)], axis=2).reshape(NFC, 128, 2048))
        out["wd%d" % n] = f(wdn.reshape(2, FH, 128, NDC, 128).transpose(0, 3, 2, 1, 4).reshape(16, 128, FH * 128))
    win = f(inp["w_in"])[0]
    out["wv"] = f(win[:, 1024:1536].reshape(2, 4, 128, 512).transpose(0, 2, 1, 3).reshape(2, 128, 2048))
    out["wz"] = f(win[:, 1536:2048].reshape(NDC, 128, 4, 128).transpose(2, 1, 0, 3).reshape(4, 128, 1024))
    out["wgrp"] = f(inp["pool_w_group"])[0]
    out["wqk"] = f(win[:, 0:1024].reshape(NDC, 128, 2, H, HD).transpose(3, 1, 2, 0, 4).reshape(H, 128, 1024))
    out["wgate"] = f(win[:, 2048:4096].reshape(NDC, 128, 2, 8, 128).transpose(3, 1, 2, 0, 4).reshape(8, 128, 2048))
    wab = np.stack([f(inp["w_branch_attn"])[0], f(inp["w_branch_pool"])[0]], axis=0)
    out["wbr"] = f(wab.reshape(2, 4, 128, 8, 128).transpose(3, 2, 0, 1, 4).reshape(8, 128, 1024))
    out["wout"] = f(f(inp["w_out"])[0].reshape(NDC, 128, 8, 128).transpose(2, 1, 0, 3).reshape(8, 128, 1024))
    g = np.stack([f(inp["ffn1_norm"])[0], f(inp["mix_norm"])[0], f(inp["ffn2_norm"])[0], f(inp["final_norm"])], axis=0)
    out["gains"] = f(g.reshape(4, NDC, 128).transpose(2, 0, 1).reshape(128, 4 * NDC))
    out["pscale"] = f(f(inp["pool_scale"])[0].reshape(4, 128).T)
    out["rpb"] = f(inp["rpb_table"])
    out.update(_host_consts())
    return out


_CACHE = {}


def _run(inputs, stop_after="all", ncores=NB):
    if stop_after not in _CACHE:
        _CACHE[stop_after] = build_program(stop_after)
    nc = _CACHE[stop_after]
    shared = _prep_shared(inputs)
    x = np.asarray(inputs["x"], dtype=np.float32)
    in_maps = []
    for b in range(ncores):
        m = dict(shared)
        m["xT"] = np.ascontiguousarray(x[b].reshape(S, NDC, 128).transpose(2, 1, 0))
        in_maps.append(m)
    res = run_bass_kernel_spmd(nc, in_maps, core_ids=list(range(ncores)))
    out = np.zeros((NB, S, D), np.float32)
    for b in range(ncores):
        out[b] = np.asarray(res.results[b]["outT"]).transpose(2, 1, 0).reshape(S, D)
    return out


def kernel(**inputs):
    return _run(inputs, "all")
```

```python
import math
from contextlib import ExitStack

import numpy as np
import ml_dtypes

import concourse.bass as bass
import concourse.mybir as mybir
from concourse.bass_utils import run_bass_kernel_spmd

F32 = mybir.dt.float32
BF16 = mybir.dt.bfloat16
U8 = mybir.dt.uint8
AF = mybir.ActivationFunctionType
ALU = mybir.AluOpType
AX = mybir.AxisListType

D = 1024
S = 2048
NB = 8
DFF = 2816
NFC = DFF // 128
FH = 11
NDC = D // 128
NTT = S // 512
H = 8
HD = 64
BLK = 256
EPS = 1e-6
NEG = -30000.0
RING_SLOTS = 5
SLOT_BYTES = 4096
PF = 3
FW = 1152
TL = 1280


class Reg:
    __slots__ = ("name", "w", "r")

    def __init__(self, name):
        self.name = name
        self.w = None
        self.r = {}


class Prog:
    ENGS = ("pe", "act", "dve", "pool", "sp")

    def __init__(self):
        self.lists = {e: [] for e in self.ENGS}
        self.cnt = {e: 0 for e in self.ENGS}
        self.waited = {e: {} for e in self.ENGS}
        self.same_wait = {"pe": False, "act": True, "dve": True, "pool": True, "sp": False}
        self.psem = {e: "p_" + e for e in self.ENGS}
        self.dcnt = {}
        self.sem_names = set(self.psem.values())

    def _deps(self, reads, writes):
        deps = {}

        def add(t):
            if t is None:
                return
            s, v = t
            if deps.get(s, 0) < v:
                deps[s] = v

        for r in reads:
            add(r.w)
        for w in writes:
            add(w.w)
            for s, v in w.r.items():
                add((s, v))
        return deps

    def _wait(self, e, deps, skip=None):
        for s, v in deps.items():
            if s == skip:
                continue
            if s == self.psem[e] and not self.same_wait[e]:
                continue
            if self.waited[e].get(s, 0) < v:
                self.lists[e].append(("wait", s, v))
                self.waited[e][s] = v

    def _update(self, tok, reads, writes):
        s, v = tok
        for r in reads:
            if r.r.get(s, 0) < v:
                r.r[s] = v
        for w in writes:
            w.w = tok
            w.r = {}

    def emit(self, e, fns, reads=(), writes=()):
        if not isinstance(fns, (list, tuple)):
            fns = [fns]
        self._wait(e, self._deps(reads, writes))
        for f in fns[:-1]:
            self.lists[e].append(("ins", f, None, 0))
        self.cnt[e] += 1
        self.lists[e].append(("ins", fns[-1], self.psem[e], 1))
        tok = (self.psem[e], self.cnt[e])
        self._update(tok, reads, writes)
        return tok

    def dma(self, q, fn, sem, reads=(), writes=()):
        self.sem_names.add(sem)
        self._wait(q, self._deps(reads, writes), skip=sem)
        self.dcnt[sem] = self.dcnt.get(sem, 0) + 16
        self.lists[q].append(("ins", fn, sem, 16))
        tok = (sem, self.dcnt[sem])
        self._update(tok, reads, writes)
        return tok

    def barrier(self, engs=("pe", "act", "dve", "sp"), dma_sems=()):
        for e in engs:
            deps = {}
            for o in engs:
                if o != e and self.cnt[o] > 0:
                    deps[self.psem[o]] = self.cnt[o]
            for s in dma_sems:
                if self.dcnt.get(s, 0) > 0:
                    deps[s] = self.dcnt[s]
            self._wait(e, deps)

    def wait_all(self, e, sems):
        deps = {s: self.dcnt[s] for s in sems if self.dcnt.get(s, 0) > 0}
        self._wait(e, deps)


def _rpb_bucket_np(dist):
    n = np.maximum(dist, 0)
    max_exact = 16
    nf = np.maximum(n, 1).astype(np.float32)
    large = max_exact + (np.log(nf / np.float32(max_exact)) / np.float32(math.log(128 / max_exact))
                         * np.float32(32 - max_exact)).astype(np.int32)
    large = np.minimum(large, 31)
    return np.where(n < max_exact, n, large)


def _host_consts():
    oh = np.zeros((33, FW), np.float32)
    for i in range(FW):
        d = i - 511
        if d < 0 or i >= 1151:
            oh[32, i] = 1.0
        else:
            oh[int(_rpb_bucket_np(np.array([d], np.int32))[0]), i] = 1.0
    kblk = np.zeros((8, S), np.float32)
    for j in range(8):
        kblk[j, j * BLK:(j + 1) * BLK] = 1.0
    ssum = np.zeros((64, 4, 72), np.float32)
    for ib in range(4, 8):
        for j in range(ib):
            for jp in range(ib):
                ssum[j * 8 + jp, ib - 4, 64 + j] = 1.0
    rc = np.zeros((128, 16), np.float32)
    rc[:, :] = 1.0 / np.arange(1, 17, dtype=np.float32)[None, :]
    return {
        "c_onehot": oh,
        "c_kblk": kblk.astype(ml_dtypes.bfloat16),
        "c_ssum": ssum.reshape(64, 4 * 72).astype(ml_dtypes.bfloat16),
        "c_rc": rc,
    }


def build_program(stop_after="all"):
    nc = bass.Bass("TRN2", target_bir_lowering=False)
    P = Prog()

    def din(name, shape, dt=F32):
        return nc.dram_tensor(name, list(shape), dt, kind="ExternalInput").ap()

    xT = din("xT", [128, NDC, S])
    gains = din("gains", [128, 4 * NDC])
    pscale_d = din("pscale", [128, 4])
    rpb_d = din("rpb", [32, H])
    wgu_d = [din("wgu%d" % n, [NFC, 128, 2048]) for n in (1, 2)]
    wd_d = [din("wd%d" % n, [16, 128, FH * 128]) for n in (1, 2)]
    wv_d = din("wv", [2, 128, 2048])
    wz_d = din("wz", [4, 128, 1024])
    wgrp_d = din("wgrp", [4, 128, 128])
    wqk_d = din("wqk", [H, 128, 1024])
    wgate_d = din("wgate", [8, 128, 2048])
    wbr_d = din("wbr", [8, 128, 1024])
    wout_d = din("wout", [8, 128, 1024])
    c_onehot = din("c_onehot", [33, FW])
    c_kblk = din("c_kblk", [8, S], BF16)
    c_ssum = din("c_ssum", [64, 4 * 72], BF16)
    c_rc = din("c_rc", [128, 16])
    outT = nc.dram_tensor("outT", [128, NDC, S], F32, kind="ExternalOutput").ap()
    fd_t = nc.dram_tensor("fd_scr", [H, FW], BF16, kind="Internal")
    tsk_t = nc.dram_tensor("tsk_scr", [H, 128 * TL], BF16, kind="Internal")

    R1_BYTES = 49152
    SCR_BYTES = 16384
    off = {}
    cur = [0]

    def region(name, nbytes):
        off[name] = cur[0]
        cur[0] += (nbytes + 31) // 32 * 32

    region("h", 65536)
    region("u", 32768)
    region("r1", R1_BYTES)
    region("aT", 16384)
    region("ring", RING_SLOTS * SLOT_BYTES)
    region("rt", 2048)
    region("rstd", 2048)
    region("sq", 3 * 1024)
    region("scr", SCR_BYTES)
    region("gains", 128)
    region("pscale", 32)
    region("onesD", 256)
    region("ones64", 128)
    region("cbias", 32)
    region("rc", 64)
    region("tab33", 32)
    region("ssum", 576)
    TOTAL = cur[0]
    big = nc.alloc_sbuf_tensor("big", [128, TOTAL], U8)

    def view(base, nbytes, dt, pattern=None, **kw):
        v = big[:, base:base + nbytes].bitcast(dt)
        if pattern:
            v = v.rearrange(pattern, **kw)
        return v

    h = view(off["h"], 65536, F32, "p (a t) -> p a t", a=NDC)
    u = view(off["u"], 32768, BF16, "p (a t) -> p a t", a=NDC)
    r1 = off["r1"]
    A = view(r1, FH * 4096, BF16, "p (a t) -> p a t", a=FH)
    pT = view(r1, 16384, BF16, "p (a t) -> p a t", a=4)
    Va = view(r1 + 16384, 16640, BF16, "p (i c) -> p i c", i=16)
    QaB = [view(r1 + 33024, 4096, BF16), view(r1, 4096, BF16)]
    KaB = [view(r1 + 37120, 4096, BF16), view(r1 + 4096, 4096, BF16)]
    G2 = [view(r1 + 41216 + i * 2048, 2048, BF16) for i in range(2)]
    merged = view(r1 + 16384, 32768, BF16, "p (a t) -> p a t", a=NDC)
    of32 = [view(r1 + i * 16384, 16384, F32, "p (a t) -> p a t", a=NDC) for i in range(2)]
    aT = view(off["aT"], 16384, BF16, "p (a t) -> p a t", a=4)
    zf = view(r1 + 16384, 8192, F32)
    ta = view(r1 + 24576, 8192, F32)
    tb = view(r1 + 32768, 8192, F32)
    ta_bf = view(r1 + 24576, 4096, BF16)
    tb_bf = view(r1 + 32768, 4096, BF16)
    ring = [view(off["ring"] + i * SLOT_BYTES, SLOT_BYTES, BF16) for i in range(RING_SLOTS)]
    rt = view(off["rt"], 2048, F32)
    rstd = view(off["rstd"], 2048, F32)
    sq = [view(off["sq"] + i * 1024, 1024, BF16) for i in range(3)]
    scr = off["scr"]
    gains_sb = view(off["gains"], 128, F32)
    pscale_sb = view(off["pscale"], 16, F32)
    onesD = view(off["onesD"], 256, BF16)
    ones64 = view(off["ones64"], 128, BF16)
    cbias = view(off["cbias"], 32, F32)
    rc_sb = view(off["rc"], 64, F32)
    tab33 = view(off["tab33"], 32, F32)
    ssum_sb = view(off["ssum"], 576, BF16, "p (a c) -> p a c", a=4)
    sg = [view(scr + i * 1024, 1024, BF16) for i in range(3)]
    onehot_sb = view(off["aT"], FW * 4, F32)
    fexp_sb = view(off["aT"] + FW * 4, FW * 2, BF16)
    Pt = [view(scr + i * 1024, 1024, BF16) for i in range(6)]
    Et = [view(scr + 6144 + i * 1024, 1024, BF16) for i in range(2)]
    dhi = [view(scr + 8192 + i * 1024, 1024, BF16) for i in range(2)]
    dlo = [view(scr + 10240 + i * 1024, 1024, BF16) for i in range(2)]
    rec = [view(scr + 12288, 2048, F32)]
    stg = [view(scr + 14336 + i * 1024, 1024, BF16) for i in range(2)]
    ind = [view(r1 + 45312 + i * 1024, 1024, BF16) for i in range(2)]
    kmT = view(r1 + 47360, 32, F32)
    diffT = view(r1 + 47392, 128, BF16)
    tmp16 = view(scr, 64, F32)
    sgm = [view(scr + i * 2048, 2048, F32) for i in range(2)]
    m1 = [view(scr + 4096 + i * 2048, 2048, F32) for i in range(2)]

    bank = [nc.alloc_psum_tensor("bank%d" % i, [128, 512], F32) for i in range(8)]
    bank_reg = [Reg("bank%d" % i) for i in range(8)]

    h_reg = [[Reg("h%d_%d" % (dc, tt)) for tt in range(NTT)] for dc in range(NDC)]
    u_reg = [Reg("u%d" % tt) for tt in range(NTT)]
    A_reg = [[Reg("A%d_%d" % (f, tt)) for tt in range(NTT)] for f in range(FH)]
    sq_reg = [Reg("sq%d" % i) for i in range(3)]
    rt_reg = Reg("rt")
    rstd_reg = Reg("rstd")
    sg_reg = [Reg("sg%d" % i) for i in range(3)]
    slot_reg = [Reg("slot%d" % i) for i in range(RING_SLOTS)]
    const_reg = Reg("consts")
    G_reg = Reg("G")

    slabs = []

    def ffn_slabs(n):
        for fh in range(2):
            for fcl in range(FH):
                slabs.append([(wgu_d[n][fh * FH + fcl], 0, 2048)])
            for dco in range(NDC):
                slabs.append([(wd_d[n][fh * NDC + dco], 0, FH * 128)])

    ffn_slabs(0)
    slabs.append([(wv_d[0], 0, 2048)])
    slabs.append([(wv_d[1], 0, 2048)])
    for hh in range(H):
        slabs.append([(wqk_d[hh], 0, 1024)])
    for g in range(4):
        slabs.append([(wz_d[g], 0, 1024), (wgrp_d[g], 1024, 128)])
    for c in range(8):
        slabs.append([(wgate_d[c], 0, 2048)])
        slabs.append([(wbr_d[c], 0, 1024)])
    for c in range(8):
        slabs.append([(wout_d[c], 0, 1024)])
    ffn_slabs(1)
    state = {"issued": 0, "next": 0}

    def issue_slabs(upto):
        while state["issued"] < min(upto, len(slabs)):
            i = state["issued"]
            s = i % RING_SLOTS
            for (src, o, n) in slabs[i]:
                dst = ring[s][:, o:o + n]
                P.dma("pool", (lambda g, dst=dst, src=src: g.dma_start(out=dst, in_=src)),
                      "ring%d" % s, writes=[slot_reg[s]])
            state["issued"] += 1

    def next_slab():
        i = state["next"]
        state["next"] += 1
        issue_slabs(i + PF + 1)
        s = i % RING_SLOTS
        return ring[s], slot_reg[s]

    def mm(out, lhsT, rhs, start, stop):
        return lambda t: t.matmul(out, lhsT, rhs, start=start, stop=stop)

    sqi = [0]

    def rmsnorm(n, tt, dst_fn, dst_regs, ssq_bank):
        ts = slice(tt * 512, (tt + 1) * 512)
        for dc in range(NDC):
            k = sqi[0] % 3
            sqi[0] += 1
            P.emit("act", (lambda a, k=k, dc=dc: a.activation(out=sq[k][:, :], in_=h[:, dc, ts], func=AF.Square)),
                   reads=[h_reg[dc][tt]], writes=[sq_reg[k]])
            P.emit("pe", mm(bank[ssq_bank][:, :], onesD[:, :], sq[k][:, :], dc == 0, dc == NDC - 1),
                   reads=[sq_reg[k]], writes=[bank_reg[ssq_bank]])
        P.emit("act", (lambda a: a.activation(out=rt[:, :], in_=bank[ssq_bank][:, :], func=AF.Sqrt, bias=EPS, scale=1.0)),
               writes=[bank_reg[ssq_bank], rt_reg])
        P.emit("dve", (lambda v: v.reciprocal(out=rstd[:, :], in_=rt[:, :])), reads=[rt_reg], writes=[rstd_reg])
        for dc in range(NDC):
            gcol = gains_sb[:, n * NDC + dc:n * NDC + dc + 1]
            P.emit("dve", (lambda v, dc=dc, gcol=gcol: v.scalar_tensor_tensor(
                out=dst_fn(dc), in0=h[:, dc, ts], scalar=gcol, in1=rstd[:, :], op0=ALU.mult, op1=ALU.mult)),
                reads=[h_reg[dc][tt], rstd_reg, const_reg], writes=dst_regs(dc))

    def ffn(n, gain_idx):
        for tt in range(NTT):
            ts = slice(tt * 512, (tt + 1) * 512)
            rmsnorm(gain_idx, tt, (lambda dc, ts=ts: u[:, dc, ts]), (lambda dc, tt=tt: [u_reg[tt]]), 6 + tt % 2)
        k = 0
        kk = 0
        for fh in range(2):
            for fcl in range(FH):
                slab, sreg = next_slab()
                wg = slab[:, 0:1024].rearrange("p (a c) -> p a c", a=NDC)
                wu = slab[:, 1024:2048].rearrange("p (a c) -> p a c", a=NDC)
                for tt in range(NTT):
                    ts = slice(tt * 512, (tt + 1) * 512)
                    gb = k % 2
                    ub = 2 + k % 2
                    si = k % 3
                    k += 1
                    P.emit("pe", [mm(bank[gb][:, :], wg[:, dc, :], u[:, dc, ts], dc == 0, dc == NDC - 1) for dc in range(NDC)],
                           reads=[u_reg[tt], sreg], writes=[bank_reg[gb]])
                    P.emit("pe", [mm(bank[ub][:, :], wu[:, dc, :], u[:, dc, ts], dc == 0, dc == NDC - 1) for dc in range(NDC)],
                           reads=[u_reg[tt], sreg], writes=[bank_reg[ub]])
                    P.emit("act", (lambda a, gb=gb, si=si: a.activation(out=sg[si][:, :], in_=bank[gb][:, :], func=AF.Silu)),
                           writes=[bank_reg[gb], sg_reg[si]])
                    P.emit("dve", (lambda v, ub=ub, si=si, fcl=fcl, ts=ts: v.tensor_tensor(
                        out=A[:, fcl, ts], in0=bank[ub][:, :], in1=sg[si][:, :], op=ALU.mult)),
                        reads=[sg_reg[si]], writes=[bank_reg[ub], A_reg[fcl][tt]])
            for dco in range(NDC):
                slab, sreg = next_slab()
                wdv = slab[:, 0:FH * 128].rearrange("p (a c) -> p a c", a=FH)
                for tt in range(NTT):
                    ts = slice(tt * 512, (tt + 1) * 512)
                    yb = 4 + kk % 2
                    kk += 1
                    P.emit("pe", [mm(bank[yb][:, :], wdv[:, f, :], A[:, f, ts], f == 0, f == FH - 1) for f in range(FH)],
                           reads=[A_reg[f][tt] for f in range(FH)] + [sreg], writes=[bank_reg[yb]])
                    P.emit("dve", (lambda v, yb=yb, dco=dco, ts=ts: v.scalar_tensor_tensor(
                        out=h[:, dco, ts], in0=bank[yb][:, :], scalar=0.5, in1=h[:, dco, ts], op0=ALU.mult, op1=ALU.add)),
                        writes=[bank_reg[yb], h_reg[dco][tt]])

    for tt in range(NTT):
        ts = slice(tt * 512, (tt + 1) * 512)
        P.dma("sp", (lambda q, ts=ts: q.dma_start(out=h[:, :, ts], in_=xT[:, :, ts])), "x%d" % tt,
              writes=[h_reg[dc][tt] for dc in range(NDC)])
    for (dst, src) in (
        (gains_sb[:, :], gains),
        (pscale_sb[:, :], pscale_d),
        (rc_sb[:, :], c_rc),
        (ssum_sb[0:64, :, :], c_ssum.rearrange("p (a c) -> p a c", a=4)),
        (tab33[0:32, :], rpb_d),
        (cbias[:, :], rpb_d[31:32, :].partition_broadcast(128)),
        (onehot_sb[0:33, :], c_onehot),
    ):
        P.dma("sp", (lambda q, dst=dst, src=src: q.dma_start(out=dst, in_=src)), "cst", writes=[const_reg])
    P.emit("dve", (lambda v: v.memset(onesD[:, :], 1.0 / D)), writes=[const_reg])
    P.emit("dve", (lambda v: v.memset(ones64[:, :], 1.0)), writes=[const_reg])
    P.emit("dve", (lambda v: v.memset(tab33[32:33, :], NEG)), writes=[const_reg])
    for e in ("pe", "act", "dve"):
        P.wait_all(e, ["cst"])
    P.barrier(engs=("pe", "act", "dve"))

    fb = 0
    for (c0, c1) in ((0, 512), (512, 1024), (1024, FW)):
        P.emit("pe", mm(bank[fb][0:8, 0:c1 - c0], tab33[0:33, 0:8], onehot_sb[0:33, c0:c1], True, True), writes=[bank_reg[fb]])
        P.emit("act", (lambda a, c0=c0, c1=c1: a.activation(out=fexp_sb[0:8, c0:c1], in_=bank[fb][0:8, 0:c1 - c0], func=AF.Exp)),
               writes=[bank_reg[fb], G_reg])
    P.dma("sp", (lambda q: q.dma_start(out=fd_t.ap(), in_=fexp_sb[0:8, :])), "g1", reads=[G_reg], writes=[G_reg])
    for hh in range(H):
        src = bass.AP(fd_t, hh * FW, [[0, 128], [1, 1151]])
        dst = bass.AP(tsk_t, hh * 128 * TL, [[TL + 1, 128], [1, 1151]])
        P.dma("sp", (lambda q, dst=dst, src=src: q.dma_start(out=dst, in_=src)), "g2", reads=[G_reg], writes=[])
    G2_reg = Reg("G2")
    G2_reg.w = ("g2", P.dcnt["g2"])
    Gh_reg = [Reg("Gh0"), Reg("Gh1")]

    def load_G(hh):
        src = bass.AP(tsk_t, hh * 128 * TL + 127, [[TL, 128], [1, 1024]])
        P.dma("sp", (lambda q: q.dma_start(out=G2[hh % 2][:, :], in_=src)), "gh%d" % (hh % 2),
              reads=[G2_reg], writes=[Gh_reg[hh % 2]])

    def out_h_raw():
        for tt in range(NTT):
            ts = slice(tt * 512, (tt + 1) * 512)
            P.dma("sp", (lambda q, ts=ts: q.dma_start(out=outT[:, :, ts], in_=h[:, :, ts])), "st",
                  reads=[h_reg[dc][tt] for dc in range(NDC)])
        P.wait_all("sp", ["st"])

    def done():
        replay(nc, P)
        return nc

    ffn(0, 0)
    if stop_after == "ffn1":
        out_h_raw()
        return done()

    P.barrier()
    for tt in range(NTT):
        ts = slice(tt * 512, (tt + 1) * 512)
        rmsnorm(1, tt, (lambda dc, ts=ts: u[:, dc, ts]), (lambda dc, tt=tt: [u_reg[tt]]), 6 + tt % 2)

    Va4 = Va.rearrange("p i (h c) -> p i h c", c=65)
    Va_reg = [Reg("Va%d" % i) for i in range(16)]
    s0, sr0 = next_slab()
    s1, sr1 = next_slab()
    wv = [s0.rearrange("p (a c) -> p a c", a=4), s1.rearrange("p (a c) -> p a c", a=4)]
    P.emit("dve", (lambda v: v.memset(Va4[:, :, :, 64:65], 1.0)), writes=Va_reg)
    for i in range(16):
        b = i % 4
        tsl = slice(i * 128, (i + 1) * 128)
        P.emit("pe", [mm(bank[b][:, :], u[:, dc, tsl], wv[dc // 4][:, dc % 4, :], dc == 0, dc == NDC - 1) for dc in range(NDC)],
               reads=[u_reg[i // 4], sr0, sr1], writes=[bank_reg[b]])
        src = bank[b][:, :].rearrange("p (h c) -> p h c", c=64)
        if i % 2 == 0:
            P.emit("act", (lambda a, i=i, src=src: a.activation(out=Va4[:, i, :, 0:64], in_=src, func=AF.Copy)),
                   writes=[bank_reg[b], Va_reg[i]])
        else:
            P.emit("dve", (lambda v, i=i, src=src: v.tensor_copy(out=Va4[:, i, :, 0:64], in_=src)),
                   writes=[bank_reg[b], Va_reg[i]])

    Kb_reg = [Reg("Kb0"), Reg("Kb1")]
    Qz_reg = [Reg("Qz0"), Reg("Qz1")]
    Qm_reg = [[Reg("Qm%d_%d" % (b, ib)) for ib in range(8)] for b in range(2)]
    Qa_reg = [[Reg("Qa%d_%d" % (b, tt)) for tt in range(NTT)] for b in range(2)]
    Ka_reg = [[Reg("Ka%d_%d" % (b, tt)) for tt in range(NTT)] for b in range(2)]
    km_reg = Reg("km")
    df_reg = Reg("df")
    ind_reg = [Reg("ind0"), Reg("ind1")]
    Pt_reg = [Reg("Pt%d" % i) for i in range(6)]
    Et_reg = [Reg("Et%d" % i) for i in range(2)]
    dhi_reg = [Reg("dhi%d" % i) for i in range(2)]
    dlo_reg = [Reg("dlo%d" % i) for i in range(2)]
    rec_reg = [Reg("rec0")]
    stg_reg = [Reg("stg%d" % i) for i in range(2)]
    aT_reg = [[Reg("aT%d_%d" % (hh, m)) for m in range(NTT)] for hh in range(H)]
    for b in range(2):
        P.dma("sp", (lambda q, b=b: q.dma_start(out=KaB[b][64:72, :], in_=c_kblk)), "cst2", writes=[Kb_reg[b]])
        P.emit("dve", (lambda v, b=b: v.memset(QaB[b][64:72, :], 0.0)), writes=[Qz_reg[b]] + Qm_reg[b])
    cnts = {"s": 0, "p": 0, "e": 0, "f": 0, "g": 0, "o": 0}
    SCALE = HD ** -0.5
    load_G(0)

    pj = {"k": 0}

    def proj_groups(hh):
        b = hh % 2
        slab, sreg = next_slab()
        wq = slab[:, 0:512].rearrange("p (a c) -> p a c", a=NDC)
        wk = slab[:, 512:1024].rearrange("p (a c) -> p a c", a=NDC)
        out = []
        for tt in range(NTT):
            ts = slice(tt * 512, (tt + 1) * 512)
            for (wm, dstb, dreg) in ((wq, QaB[b], Qa_reg[b][tt]), (wk, KaB[b], Ka_reg[b][tt])):
                pb = (0, 2)[pj["k"] % 2]
                pj["k"] += 1
                for dc in range(NDC):
                    def one(wm=wm, ts=ts, tt=tt, dc=dc, pb=pb):
                        P.emit("pe", mm(bank[pb][0:64, :], wm[:, dc, :], u[:, dc, ts], dc == 0, dc == NDC - 1),
                               reads=[u_reg[tt], sreg], writes=[bank_reg[pb]])
                    out.append(one)

                def evac(dstb=dstb, dreg=dreg, ts=ts, pb=pb):
                    P.emit("dve", (lambda v: v.tensor_copy(out=dstb[0:64, ts], in_=bank[pb][0:64, :])),
                           writes=[bank_reg[pb], dreg])
                out.append(evac)
        return out

    def attn_head(hh, nxt):
        b = hh % 2
        Qa, Ka = QaB[b], KaB[b]
        if hh + 1 < H:
            load_G(hh + 1)
        Gt = G2[hh % 2]
        Greg = Gh_reg[hh % 2]
        P.emit("dve", (lambda v: v.reduce_sum(out=kmT[0:64, 0:8], in_=Ka[0:64, :].rearrange("p (j k) -> p j k", k=BLK), axis=AX.X)),
               reads=Ka_reg[b], writes=[km_reg])
        P.emit("dve", (lambda v: v.tensor_tensor(
            out=diffT[0:64, :].rearrange("p (a b) -> p a b", b=8),
            in0=kmT[0:64, 0:8].unsqueeze(1).broadcast_to([64, 8, 8]),
            in1=kmT[0:64, 0:8].unsqueeze(2).broadcast_to([64, 8, 8]), op=ALU.subtract)),
            reads=[km_reg], writes=[df_reg])

        def selection():
            for half in range(2):
                qs = slice(1024 + half * 512, 1536 + half * 512)
                P.emit("pe", mm(bank[1][0:64, :], diffT[0:64, 0:64], Qa[0:64, qs], True, True),
                       reads=[df_reg, Qa_reg[b][2 + half]], writes=[bank_reg[1]])
                P.emit("dve", (lambda v, half=half: v.tensor_single_scalar(
                    out=ind[half][0:64, :], in_=bank[1][0:64, :], scalar=0.0, op=ALU.is_gt)),
                    writes=[bank_reg[1], ind_reg[half]])
                for blk in range(2):
                    ib = 4 + half * 2 + blk
                    P.emit("pe", mm(bank[1][0:72, 0:256], ssum_sb[0:64, ib - 4, :], ind[half][0:64, blk * 256:(blk + 1) * 256], True, True),
                           reads=[ind_reg[half]], writes=[bank_reg[1]])
                    P.emit("dve", (lambda v, ib=ib: v.tensor_scalar(
                        out=Qa[64:72, ib * BLK:(ib + 1) * BLK], in0=bank[1][64:72, 0:256],
                        scalar1=2.5, scalar2=NEG, op0=ALU.is_gt, op1=ALU.mult)),
                        writes=[bank_reg[1], Qm_reg[b][ib]])

        tiles = [(m, kt) for m in range(NTT) for kt in range(4 * m + 4)]
        info = {}
        pending = []
        DEPTH = 3
        DELAY = 4
        SRING = (3, 4, 5)

        def s_stage(idx):
            m, kt = tiles[idx]
            q0 = m * 512
            sb = SRING[cnts["s"] % len(SRING)]
            cnts["s"] += 1
            pi = cnts["p"] % 6
            cnts["p"] += 1
            info[idx] = pi
            delta = kt * 128 - q0
            cs = max(0, delta)
            rd = [Ka_reg[b][kt // 4], Kb_reg[b], Qa_reg[b][m]] + ([Qz_reg[b]] if m < 2 else [Qm_reg[b][2 * m], Qm_reg[b][2 * m + 1]])
            P.emit("pe", mm(bank[sb][:, cs:512], Ka[0:72, kt * 128:(kt + 1) * 128], Qa[0:72, q0 + cs:q0 + 512], True, True),
                   reads=rd, writes=[bank_reg[sb]])
            if delta < -128:
                P.emit("act", (lambda a, sb=sb, pi=pi: a.activation(
                    out=Pt[pi][:, :], in_=bank[sb][:, :], func=AF.Exp, bias=cbias[:, hh:hh + 1], scale=SCALE)),
                    writes=[bank_reg[sb], Pt_reg[pi]])
            else:
                ei = cnts["e"] % 2
                cnts["e"] += 1
                goff = 384 - delta
                P.emit("act", (lambda a, sb=sb, ei=ei, cs=cs: a.activation(
                    out=Et[ei][:, cs:512], in_=bank[sb][:, cs:512], func=AF.Exp, scale=SCALE)),
                    writes=[bank_reg[sb], Et_reg[ei]])
                eng = "pool" if cnts["o"] % 2 == 1 else "dve"
                cnts["o"] += 1
                P.emit(eng, (lambda g, ei=ei, pi=pi, goff=goff, cs=cs: g.tensor_tensor(
                    out=Pt[pi][:, cs:512], in0=Et[ei][:, cs:512], in1=Gt[:, goff + cs:goff + 512], op=ALU.mult)),
                    reads=[Et_reg[ei], Greg], writes=[Pt_reg[pi]])

        def pv_stage(idx):
            m, kt = tiles[idx]
            q0 = m * 512
            pi = info[idx]
            ob = 6 + m % 2
            last = 4 * m + 3
            cs = max(0, kt * 128 - q0)
            P.emit("pe", mm(bank[ob][0:65, cs:512], Va4[:, kt, hh, :], Pt[pi][:, cs:512], kt == 0, kt == last),
                   reads=[Va_reg[kt], Pt_reg[pi]], writes=[bank_reg[ob]])
            if kt != last:
                return
            fi = cnts["f"] % 2
            cnts["f"] += 1
            P.emit("dve", (lambda v, ob=ob, fi=fi: v.tensor_copy(out=dhi[fi][64:65, :], in_=bank[ob][64:65, :])),
                   writes=[bank_reg[ob], dhi_reg[fi]])
            P.emit("dve", (lambda v, ob=ob, fi=fi: v.tensor_tensor(
                out=dlo[fi][64:65, :], in0=bank[ob][64:65, :], in1=dhi[fi][64:65, :], op=ALU.subtract)),
                reads=[dhi_reg[fi]], writes=[bank_reg[ob], dlo_reg[fi]])

            def stage_b(ob=ob, fi=fi, m=m, q0=q0):
                P.emit("pe", [mm(bank[1][0:64, :], ones64[64:65, 0:64], dhi[fi][64:65, :], True, False),
                              mm(bank[1][0:64, :], ones64[64:65, 0:64], dlo[fi][64:65, :], False, True)],
                       reads=[dhi_reg[fi], dlo_reg[fi]], writes=[bank_reg[1]])
                P.emit("dve", (lambda v: v.reciprocal(out=rec[0][0:64, :], in_=bank[1][0:64, :])),
                       writes=[bank_reg[1], rec_reg[0]])
                c = hh // 2
                if hh % 2 == 0:
                    P.emit("dve", (lambda v: v.tensor_tensor(
                        out=aT[0:64, c, q0:q0 + 512], in0=bank[ob][0:64, :], in1=rec[0][0:64, :], op=ALU.mult)),
                        reads=[rec_reg[0]], writes=[bank_reg[ob], aT_reg[hh][m]])
                else:
                    gi = cnts["g"] % 2
                    cnts["g"] += 1
                    P.emit("dve", (lambda v: v.tensor_tensor(
                        out=stg[gi][0:64, :], in0=bank[ob][0:64, :], in1=rec[0][0:64, :], op=ALU.mult)),
                        reads=[rec_reg[0]], writes=[bank_reg[ob], stg_reg[gi]])
                    P.dma("sp", (lambda q: q.dma_start(out=aT[64:128, c, q0:q0 + 512], in_=stg[gi][0:64, :])),
                          "stg%d" % gi, reads=[stg_reg[gi]], writes=[aT_reg[hh][m]])

            pending.append([DELAY, stage_b])

        n = len(tiles)
        for idx in range(n + DEPTH):
            if idx == 4:
                selection()
            if idx < n:
                s_stage(idx)
            if idx - DEPTH >= 0:
                pv_stage(idx - DEPTH)
            for _ in range(2):
                if nxt and idx >= 1:
                    nxt.pop(0)()
            for pb in list(pending):
                pb[0] -= 1
                if pb[0] <= 0:
                    pending.remove(pb)
                    pb[1]()
        for pb in pending:
            pb[1]()
        del pending[:]
        while nxt:
            nxt.pop(0)()

    cur = proj_groups(0)
    for g in cur:
        g()
    for hh in range(H):
        nxt = proj_groups(hh + 1) if hh + 1 < H else []
        attn_head(hh, nxt)
    if stop_after == "m3":
        for c in range(4):
            for tt in range(NTT):
                ts = slice(tt * 512, (tt + 1) * 512)
                P.emit("act", (lambda a, c=c, ts=ts: a.activation(out=h[:, c, ts], in_=aT[:, c, ts], func=AF.Copy)),
                       reads=[aT_reg[hh][tt] for hh in (2 * c, 2 * c + 1)],
                       writes=[h_reg[c][tt]])
        out_h_raw()
        return done()

    P.barrier(dma_sems=("stg0", "stg1"))
    zf_reg = [Reg("zf%d" % tt) for tt in range(NTT)]
    ta_reg = Reg("ta")
    tb_reg = Reg("tb")
    t16_reg = Reg("t16")
    pT_reg = [[Reg("pT%d_%d" % (g, tt)) for tt in range(NTT)] for g in range(4)]
    zslab = {}

    def z_proj(g):
        slab, sreg = next_slab()
        zslab[g] = (slab, sreg)
        wzv = slab[:, 0:1024].rearrange("p (a c) -> p a c", a=NDC)
        for tt in range(NTT):
            ts = slice(tt * 512, (tt + 1) * 512)
            b = (g % 2) * 4 + tt
            P.emit("pe", [mm(bank[b][:, :], wzv[:, dc, :], u[:, dc, ts], dc == 0, dc == NDC - 1) for dc in range(NDC)],
                   reads=[u_reg[tt], sreg], writes=[bank_reg[b]])

    z_proj(0)
    for g in range(4):
        w = 2 ** (g + 1)
        if g < 3:
            z_proj(g + 1)
        slab, sreg = zslab[g]
        wgg = slab[:, 1024:1152]
        for tt in range(NTT):
            ts = slice(tt * 512, (tt + 1) * 512)
            b = (g % 2) * 4 + tt
            P.emit("act", (lambda a, b=b, ts=ts: a.activation(out=zf[:, ts], in_=bank[b][:, :], func=AF.Copy)),
                   writes=[bank_reg[b], zf_reg[tt]])
        bufs = [(ta, ta_reg, ta_bf), (tb, tb_reg, tb_bf)]
        src, src_regs = zf, list(zf_reg)
        sft = 1
        idx = 0
        while sft < w:
            dst, dreg, _ = bufs[idx % 2]
            P.emit("dve", (lambda v, dst=dst, src=src, sft=sft: v.tensor_tensor(
                out=dst[:, sft:S], in0=src[:, sft:S], in1=src[:, 0:S - sft], op=ALU.add)),
                reads=src_regs, writes=[dreg])
            P.emit("dve", (lambda v, dst=dst, src=src, sft=sft: v.tensor_copy(out=dst[:, 0:sft], in_=src[:, 0:sft])),
                   reads=src_regs, writes=[dreg])
            src, src_regs = dst, [dreg]
            idx += 1
            sft *= 2
        fin, fin_regs = src, src_regs
        _, oreg, pooled = bufs[idx % 2]
        P.emit("dve", (lambda v, fin=fin, pooled=pooled, w=w: v.scalar_tensor_tensor(
            out=pooled[:, :], in0=fin[:, :], scalar=1.0 / w, in1=zf[:, :], op0=ALU.mult, op1=ALU.subtract)),
            reads=fin_regs + zf_reg, writes=[oreg])
        P.emit("dve", (lambda v, fin=fin, w=w: v.tensor_tensor(
            out=tmp16[:, 0:w - 1], in0=fin[:, 0:w - 1], in1=rc_sb[:, 0:w - 1], op=ALU.mult)),
            reads=fin_regs, writes=[t16_reg])
        P.emit("dve", (lambda v, pooled=pooled, w=w: v.tensor_tensor(
            out=pooled[:, 0:w - 1], in0=tmp16[:, 0:w - 1], in1=zf[:, 0:w - 1], op=ALU.subtract)),
            reads=[t16_reg] + zf_reg, writes=[oreg])
        for tt in range(NTT):
            ts = slice(tt * 512, (tt + 1) * 512)
            b = (g % 2) * 4 + tt
            P.emit("pe", mm(bank[b][:, :], wgg, pooled[:, ts], True, True), reads=[oreg, sreg], writes=[bank_reg[b]])
            P.emit("dve", (lambda v, b=b, g=g, ts=ts: v.tensor_single_scalar(
                out=pT[:, g, ts], in_=bank[b][:, :], scalar=pscale_sb[:, g:g + 1], op=ALU.mult)),
                writes=[bank_reg[b], pT_reg[g][tt]])

    P.barrier(dma_sems=("stg0", "stg1"))
    mg_reg = [[Reg("mg%d_%d" % (c, tt)) for tt in range(NTT)] for c in range(NDC)]
    sgm_reg = [Reg("sgm0"), Reg("sgm1")]
    m1_reg = [Reg("m10"), Reg("m11")]
    k4 = 0
    for c in range(NDC):
        slabg, sgr = next_slab()
        gaw = slabg[:, 0:1024].rearrange("p (a c) -> p a c", a=NDC)
        gpw = slabg[:, 1024:2048].rearrange("p (a c) -> p a c", a=NDC)
        slabb, sbr = next_slab()
        waw = slabb[:, 0:512].rearrange("p (a c) -> p a c", a=4)
        wpw = slabb[:, 512:1024].rearrange("p (a c) -> p a c", a=4)
        for tt in range(NTT):
            ts = slice(tt * 512, (tt + 1) * 512)
            b0 = 4 * (k4 % 2)
            k4 += 1
            a_rd = [aT_reg[hh][tt] for hh in range(H)]
            P.emit("pe", [mm(bank[b0][:, :], gaw[:, dc, :], u[:, dc, ts], dc == 0, dc == NDC - 1) for dc in range(NDC)],
                   reads=[u_reg[tt], sgr], writes=[bank_reg[b0]])
            P.emit("pe", [mm(bank[b0 + 1][:, :], waw[:, cc, :], aT[:, cc, ts], cc == 0, cc == 3) for cc in range(4)],
                   reads=a_rd + [sbr], writes=[bank_reg[b0 + 1]])
            P.emit("pe", [mm(bank[b0 + 2][:, :], gpw[:, dc, :], u[:, dc, ts], dc == 0, dc == NDC - 1) for dc in range(NDC)],
                   reads=[u_reg[tt], sgr], writes=[bank_reg[b0 + 2]])
            P.emit("pe", [mm(bank[b0 + 3][:, :], wpw[:, g, :], pT[:, g, ts], g == 0, g == 3) for g in range(4)],
                   reads=[pT_reg[g][tt] for g in range(4)] + [sbr], writes=[bank_reg[b0 + 3]])
            P.emit("act", (lambda a, b0=b0: a.activation(out=sgm[0][:, :], in_=bank[b0][:, :], func=AF.Sigmoid)),
                   writes=[bank_reg[b0], sgm_reg[0]])
            P.emit("act", (lambda a, b0=b0: a.activation(out=sgm[1][:, :], in_=bank[b0 + 2][:, :], func=AF.Sigmoid)),
                   writes=[bank_reg[b0 + 2], sgm_reg[1]])
            P.emit("dve", (lambda v, b0=b0: v.tensor_tensor(out=m1[0][:, :], in0=bank[b0 + 1][:, :], in1=sgm[0][:, :], op=ALU.mult)),
                   reads=[sgm_reg[0]], writes=[bank_reg[b0 + 1], m1_reg[0]])
            P.emit("dve", (lambda v, b0=b0: v.tensor_tensor(out=m1[1][:, :], in0=bank[b0 + 3][:, :], in1=sgm[1][:, :], op=ALU.mult)),
                   reads=[sgm_reg[1]], writes=[bank_reg[b0 + 3], m1_reg[1]])
            P.emit("dve", (lambda v, c=c, ts=ts: v.tensor_tensor(out=merged[:, c, ts], in0=m1[0][:, :], in1=m1[1][:, :], op=ALU.add)),
                   reads=[m1_reg[0], m1_reg[1]], writes=[mg_reg[c][tt]])
    ko = 0
    for co in range(NDC):
        slab, sreg = next_slab()
        wo = slab[:, 0:1024].rearrange("p (a c) -> p a c", a=NDC)
        for tt in range(NTT):
            ts = slice(tt * 512, (tt + 1) * 512)
            b = ko % 2
            ko += 1
            P.emit("pe", [mm(bank[b][:, :], wo[:, c, :], merged[:, c, ts], c == 0, c == NDC - 1) for c in range(NDC)],
                   reads=[mg_reg[c][tt] for c in range(NDC)] + [sreg], writes=[bank_reg[b]])
            P.emit("dve", (lambda v, b=b, co=co, ts=ts: v.tensor_tensor(out=h[:, co, ts], in0=bank[b][:, :], in1=h[:, co, ts], op=ALU.add)),
                   writes=[bank_reg[b], h_reg[co][tt]])
    if stop_after == "m4":
        out_h_raw()
        return done()

    P.barrier()
    ffn(1, 2)
    P.barrier()
    of_reg = [Reg("of0"), Reg("of1")]
    for tt in range(NTT):
        ts = slice(tt * 512, (tt + 1) * 512)
        o = of32[tt % 2]
        rmsnorm(3, tt, (lambda dc, o=o: o[:, dc, :]), (lambda dc, tt=tt: [of_reg[tt % 2]]), 6 + tt % 2)
        P.dma("sp", (lambda q, o=o, ts=ts: q.dma_start(out=outT[:, :, ts], in_=o[:, :, :])), "st%d" % (tt % 2),
              reads=[of_reg[tt % 2]])
    P.wait_all("sp", ["st0", "st1"])
    return done()


def replay(nc, P):
    with ExitStack() as es:
        sems = {name: es.enter_context(nc.semaphore(name)) for name in sorted(P.sem_names)}
        block = es.enter_context(nc.Block())

        def run(handle, lst):
            for item in lst:
                if item[0] == "wait":
                    handle.wait_ge(sems[item[1]], item[2])
                else:
                    ins = item[1](handle)
                    if item[2] is not None:
                        ins.then_inc(sems[item[2]], item[3])

        @block.tensor
        def _(t):
            run(t, P.lists["pe"])

        @block.scalar
        def _(a):
            run(a, P.lists["act"])

        @block.vector
        def _(v):
            run(v, P.lists["dve"])

        @block.gpsimd
        def _(g):
            run(g, P.lists["pool"])

        @block.sync
        def _(q):
            run(q, P.lists["sp"])


def _prep_shared(inp):
    f = lambda a: np.ascontiguousarray(np.asarray(a, dtype=np.float32))
    out = {}

    def gu(W):
        return W.reshape(NDC, 128, NFC, 128).transpose(2, 1, 0, 3)

    for n, pre in ((1, "ffn1"), (2, "ffn2")):
        wg = f(inp[pre + "_w_gate"])[0]
        wu = f(inp[pre + "_w_up"])[0]
        wdn = f(inp[pre + "_w_down"])[0]
        out["wgu%d" % n] = f(np.stack([gu(wg), gu(wu)], axis=2).reshape(NFC, 128, 2048))
        out["wd%d" % n] = f(wdn.reshape(2, FH, 128, NDC, 128).transpose(0, 3, 2, 1, 4).reshape(16, 128, FH * 128))
    win = f(inp["w_in"])[0]
    out["wv"] = f(win[:, 1024:1536].reshape(2, 4, 128, 512).transpose(0, 2, 1, 3).reshape(2, 128, 2048))
    out["wz"] = f(win[:, 1536:2048].reshape(NDC, 128, 4, 128).transpose(2, 1, 0, 3).reshape(4, 128, 1024))
    out["wgrp"] = f(inp["pool_w_group"])[0]
    out["wqk"] = f(win[:, 0:1024].reshape(NDC, 128, 2, H, HD).transpose(3, 1, 2, 0, 4).reshape(H, 128, 1024))
    out["wgate"] = f(win[:, 2048:4096].reshape(NDC, 128, 2, 8, 128).transpose(3, 1, 2, 0, 4).reshape(8, 128, 2048))
    wab = np.stack([f(inp["w_branch_attn"])[0], f(inp["w_branch_pool"])[0]], axis=0)
    out["wbr"] = f(wab.reshape(2, 4, 128, 8, 128).transpose(3, 2, 0, 1, 4).reshape(8, 128, 1024))
    out["wout"] = f(f(inp["w_out"])[0].reshape(NDC, 128, 8, 128).transpose(2, 1, 0, 3).reshape(8, 128, 1024))
    g = np.stack([f(inp["ffn1_norm"])[0], f(inp["mix_norm"])[0], f(inp["ffn2_norm"])[0], f(inp["final_norm"])], axis=0)
    out["gains"] = f(g.reshape(4, NDC, 128).transpose(2, 0, 1).reshape(128, 4 * NDC))
    out["pscale"] = f(f(inp["pool_scale"])[0].reshape(4, 128).T)
    out["rpb"] = f(inp["rpb_table"])
    out.update(_host_consts())
    return out


_CACHE = {}


def _run(inputs, stop_after="all", ncores=NB):
    if stop_after not in _CACHE:
        _CACHE[stop_after] = build_program(stop_after)
    nc = _CACHE[stop_after]
    shared = _prep_shared(inputs)
    x = np.asarray(inputs["x"], dtype=np.float32)
    in_maps = []
    for b in range(ncores):
        m = dict(shared)
        m["xT"] = np.ascontiguousarray(x[b].reshape(S, NDC, 128).transpose(2, 1, 0))
        in_maps.append(m)
    res = run_bass_kernel_spmd(nc, in_maps, core_ids=list(range(ncores)))
    out = np.zeros((NB, S, D), np.float32)
    for b in range(ncores):
        out[b] = np.asarray(res.results[b]["outT"]).transpose(2, 1, 0).reshape(S, D)
    return out


def kernel(**inputs):
    return _run(inputs, "all")
```

```python
import math
from contextlib import ExitStack

import numpy as np
import ml_dtypes

import concourse.bass as bass
import concourse.mybir as mybir
from concourse.bass_utils import run_bass_kernel_spmd

F32 = mybir.dt.float32
BF16 = mybir.dt.bfloat16
U8 = mybir.dt.uint8
AF = mybir.ActivationFunctionType
ALU = mybir.AluOpType
AX = mybir.AxisListType

D = 1024
S = 2048
NB = 8
DFF = 2816
NFC = DFF // 128
FH = 11
NDC = D // 128
NTT = S // 512
H = 8
HD = 64
BLK = 256
EPS = 1e-6
NEG = -30000.0
RING_SLOTS = 5
SLOT_BYTES = 4096
PF = 3
FW = 1152
TL = 1280


class Reg:
    __slots__ = ("name", "w", "r")

    def __init__(self, name):
        self.name = name
        self.w = None
        self.r = {}


class Prog:
    ENGS = ("pe", "act", "dve", "pool", "sp")

    def __init__(self):
        self.lists = {e: [] for e in self.ENGS}
        self.cnt = {e: 0 for e in self.ENGS}
        self.waited = {e: {} for e in self.ENGS}
        self.same_wait = {"pe": False, "act": True, "dve": True, "pool": True, "sp": False}
        self.psem = {e: "p_" + e for e in self.ENGS}
        self.dcnt = {}
        self.sem_names = set(self.psem.values())

    def _deps(self, reads, writes):
        deps = {}

        def add(t):
            if t is None:
                return
            s, v = t
            if deps.get(s, 0) < v:
                deps[s] = v

        for r in reads:
            add(r.w)
        for w in writes:
            add(w.w)
            for s, v in w.r.items():
                add((s, v))
        return deps

    def _wait(self, e, deps, skip=None):
        for s, v in deps.items():
            if s == skip:
                continue
            if s == self.psem[e] and not self.same_wait[e]:
                continue
            if self.waited[e].get(s, 0) < v:
                self.lists[e].append(("wait", s, v))
                self.waited[e][s] = v

    def _update(self, tok, reads, writes):
        s, v = tok
        for r in reads:
            if r.r.get(s, 0) < v:
                r.r[s] = v
        for w in writes:
            w.w = tok
            w.r = {}

    def emit(self, e, fns, reads=(), writes=()):
        if not isinstance(fns, (list, tuple)):
            fns = [fns]
        self._wait(e, self._deps(reads, writes))
        for f in fns[:-1]:
            self.lists[e].append(("ins", f, None, 0))
        self.cnt[e] += 1
        self.lists[e].append(("ins", fns[-1], self.psem[e], 1))
        tok = (self.psem[e], self.cnt[e])
        self._update(tok, reads, writes)
        return tok

    def dma(self, q, fn, sem, reads=(), writes=()):
        self.sem_names.add(sem)
        self._wait(q, self._deps(reads, writes), skip=sem)
        self.dcnt[sem] = self.dcnt.get(sem, 0) + 16
        self.lists[q].append(("ins", fn, sem, 16))
        tok = (sem, self.dcnt[sem])
        self._update(tok, reads, writes)
        return tok

    def barrier(self, engs=("pe", "act", "dve", "sp"), dma_sems=()):
        for e in engs:
            deps = {}
            for o in engs:
                if o != e and self.cnt[o] > 0:
                    deps[self.psem[o]] = self.cnt[o]
            for s in dma_sems:
                if self.dcnt.get(s, 0) > 0:
                    deps[s] = self.dcnt[s]
            self._wait(e, deps)

    def wait_all(self, e, sems):
        deps = {s: self.dcnt[s] for s in sems if self.dcnt.get(s, 0) > 0}
        self._wait(e, deps)


def _rpb_bucket_np(dist):
    n = np.maximum(dist, 0)
    max_exact = 16
    nf = np.maximum(n, 1).astype(np.float32)
    large = max_exact + (np.log(nf / np.float32(max_exact)) / np.float32(math.log(128 / max_exact))
                         * np.float32(32 - max_exact)).astype(np.int32)
    large = np.minimum(large, 31)
    return np.where(n < max_exact, n, large)


def _host_consts():
    oh = np.zeros((33, FW), np.float32)
    for i in range(FW):
        d = i - 511
        if d < 0 or i >= 1151:
            oh[32, i] = 1.0
        else:
            oh[int(_rpb_bucket_np(np.array([d], np.int32))[0]), i] = 1.0
    kblk = np.zeros((8, S), np.float32)
    for j in range(8):
        kblk[j, j * BLK:(j + 1) * BLK] = 1.0
    ssum = np.zeros((64, 4, 72), np.float32)
    for ib in range(4, 8):
        for j in range(ib):
            for jp in range(ib):
                ssum[j * 8 + jp, ib - 4, 64 + j] = 1.0
    rc = np.zeros((128, 16), np.float32)
    rc[:, :] = 1.0 / np.arange(1, 17, dtype=np.float32)[None, :]
    return {
        "c_onehot": oh,
        "c_kblk": kblk.astype(ml_dtypes.bfloat16),
        "c_ssum": ssum.reshape(64, 4 * 72).astype(ml_dtypes.bfloat16),
        "c_rc": rc,
    }


def build_program(stop_after="all"):
    nc = bass.Bass("TRN2", target_bir_lowering=False)
    P = Prog()

    def din(name, shape, dt=F32):
        return nc.dram_tensor(name, list(shape), dt, kind="ExternalInput").ap()

    xT = din("xT", [128, NDC, S])
    gains = din("gains", [128, 4 * NDC])
    pscale_d = din("pscale", [128, 4])
    rpb_d = din("rpb", [32, H])
    wgu_d = [din("wgu%d" % n, [NFC, 128, 2048]) for n in (1, 2)]
    wd_d = [din("wd%d" % n, [16, 128, FH * 128]) for n in (1, 2)]
    wv_d = din("wv", [2, 128, 2048])
    wz_d = din("wz", [4, 128, 1024])
    wgrp_d = din("wgrp", [4, 128, 128])
    wqk_d = din("wqk", [H, 128, 1024])
    wgate_d = din("wgate", [8, 128, 2048])
    wbr_d = din("wbr", [8, 128, 1024])
    wout_d = din("wout", [8, 128, 1024])
    c_onehot = din("c_onehot", [33, FW])
    c_kblk = din("c_kblk", [8, S], BF16)
    c_ssum = din("c_ssum", [64, 4 * 72], BF16)
    c_rc = din("c_rc", [128, 16])
    outT = nc.dram_tensor("outT", [128, NDC, S], F32, kind="ExternalOutput").ap()
    fd_t = nc.dram_tensor("fd_scr", [H, FW], BF16, kind="Internal")
    tsk_t = nc.dram_tensor("tsk_scr", [H, 128 * TL], BF16, kind="Internal")

    R1_BYTES = 49152
    SCR_BYTES = 16384
    off = {}
    cur = [0]

    def region(name, nbytes):
        off[name] = cur[0]
        cur[0] += (nbytes + 31) // 32 * 32

    region("h", 65536)
    region("u", 32768)
    region("r1", R1_BYTES)
    region("aT", 16384)
    region("ring", RING_SLOTS * SLOT_BYTES)
    region("rt", 2048)
    region("rstd", 2048)
    region("sq", 3 * 1024)
    region("scr", SCR_BYTES)
    region("gains", 128)
    region("pscale", 32)
    region("onesD", 256)
    region("ones64", 128)
    region("cbias", 32)
    region("rc", 64)
    region("tab33", 32)
    region("ssum", 576)
    TOTAL = cur[0]
    big = nc.alloc_sbuf_tensor("big", [128, TOTAL], U8)

    def view(base, nbytes, dt, pattern=None, **kw):
        v = big[:, base:base + nbytes].bitcast(dt)
        if pattern:
            v = v.rearrange(pattern, **kw)
        return v

    h = view(off["h"], 65536, F32, "p (a t) -> p a t", a=NDC)
    u = view(off["u"], 32768, BF16, "p (a t) -> p a t", a=NDC)
    r1 = off["r1"]
    A = view(r1, FH * 4096, BF16, "p (a t) -> p a t", a=FH)
    pT = view(r1, 16384, BF16, "p (a t) -> p a t", a=4)
    Va = view(r1 + 16384, 16640, BF16, "p (i c) -> p i c", i=16)
    QaB = [view(r1 + 33024, 4096, BF16), view(r1, 4096, BF16)]
    KaB = [view(r1 + 37120, 4096, BF16), view(r1 + 4096, 4096, BF16)]
    G2 = [view(r1 + 41216 + i * 2048, 2048, BF16) for i in range(2)]
    merged = view(r1 + 16384, 32768, BF16, "p (a t) -> p a t", a=NDC)
    of32 = [view(r1 + i * 16384, 16384, F32, "p (a t) -> p a t", a=NDC) for i in range(2)]
    aT = view(off["aT"], 16384, BF16, "p (a t) -> p a t", a=4)
    zf = view(r1 + 16384, 8192, F32)
    ta = view(r1 + 24576, 8192, F32)
    tb = view(r1 + 32768, 8192, F32)
    ta_bf = view(r1 + 24576, 4096, BF16)
    tb_bf = view(r1 + 32768, 4096, BF16)
    ring = [view(off["ring"] + i * SLOT_BYTES, SLOT_BYTES, BF16) for i in range(RING_SLOTS)]
    rt = view(off["rt"], 2048, F32)
    rstd = view(off["rstd"], 2048, F32)
    sq = [view(off["sq"] + i * 1024, 1024, BF16) for i in range(3)]
    scr = off["scr"]
    gains_sb = view(off["gains"], 128, F32)
    pscale_sb = view(off["pscale"], 16, F32)
    onesD = view(off["onesD"], 256, BF16)
    ones64 = view(off["ones64"], 128, BF16)
    cbias = view(off["cbias"], 32, F32)
    rc_sb = view(off["rc"], 64, F32)
    tab33 = view(off["tab33"], 32, F32)
    ssum_sb = view(off["ssum"], 576, BF16, "p (a c) -> p a c", a=4)
    sg = [view(scr + i * 1024, 1024, BF16) for i in range(3)]
    onehot_sb = view(off["aT"], FW * 4, F32)
    fexp_sb = view(off["aT"] + FW * 4, FW * 2, BF16)
    Pt = [view(scr + i * 1024, 1024, BF16) for i in range(6)]
    Et = [view(scr + 6144 + i * 1024, 1024, BF16) for i in range(2)]
    dhi = [view(scr + 8192 + i * 1024, 1024, BF16) for i in range(2)]
    dlo = [view(scr + 10240 + i * 1024, 1024, BF16) for i in range(2)]
    rec = [view(scr + 12288, 2048, F32)]
    stg = [view(scr + 14336 + i * 1024, 1024, BF16) for i in range(2)]
    ind = [view(r1 + 45312 + i * 1024, 1024, BF16) for i in range(2)]
    kmT = view(r1 + 47360, 32, F32)
    diffT = view(r1 + 47392, 128, BF16)
    tmp16 = view(scr, 64, F32)
    sgm = [view(scr + i * 2048, 2048, F32) for i in range(2)]
    m1 = [view(scr + 4096 + i * 2048, 2048, F32) for i in range(2)]

    bank = [nc.alloc_psum_tensor("bank%d" % i, [128, 512], F32) for i in range(8)]
    bank_reg = [Reg("bank%d" % i) for i in range(8)]

    h_reg = [[Reg("h%d_%d" % (dc, tt)) for tt in range(NTT)] for dc in range(NDC)]
    u_reg = [Reg("u%d" % tt) for tt in range(NTT)]
    A_reg = [[Reg("A%d_%d" % (f, tt)) for tt in range(NTT)] for f in range(FH)]
    sq_reg = [Reg("sq%d" % i) for i in range(3)]
    rt_reg = Reg("rt")
    rstd_reg = Reg("rstd")
    sg_reg = [Reg("sg%d" % i) for i in range(3)]
    slot_reg = [Reg("slot%d" % i) for i in range(RING_SLOTS)]
    const_reg = Reg("consts")
    G_reg = Reg("G")

    slabs = []

    def ffn_slabs(n):
        for fh in range(2):
            for fcl in range(FH):
                slabs.append([(wgu_d[n][fh * FH + fcl], 0, 2048)])
            for dco in range(NDC):
                slabs.append([(wd_d[n][fh * NDC + dco], 0, FH * 128)])

    ffn_slabs(0)
    slabs.append([(wv_d[0], 0, 2048)])
    slabs.append([(wv_d[1], 0, 2048)])
    for hh in range(H):
        slabs.append([(wqk_d[hh], 0, 1024)])
    for g in range(4):
        slabs.append([(wz_d[g], 0, 1024), (wgrp_d[g], 1024, 128)])
    for c in range(8):
        slabs.append([(wgate_d[c], 0, 2048)])
        slabs.append([(wbr_d[c], 0, 1024)])
    for c in range(8):
        slabs.append([(wout_d[c], 0, 1024)])
    ffn_slabs(1)
    state = {"issued": 0, "next": 0}

    def issue_slabs(upto):
        while state["issued"] < min(upto, len(slabs)):
            i = state["issued"]
            s = i % RING_SLOTS
            for (src, o, n) in slabs[i]:
                dst = ring[s][:, o:o + n]
                P.dma("pool", (lambda g, dst=dst, src=src: g.dma_start(out=dst, in_=src)),
                      "ring%d" % s, writes=[slot_reg[s]])
            state["issued"] += 1

    def next_slab():
        i = state["next"]
        state["next"] += 1
        issue_slabs(i + PF + 1)
        s = i % RING_SLOTS
        return ring[s], slot_reg[s]

    def mm(out, lhsT, rhs, start, stop):
        return lambda t: t.matmul(out, lhsT, rhs, start=start, stop=stop)

    sqi = [0]

    def rmsnorm(n, tt, dst_fn, dst_regs, ssq_bank):
        ts = slice(tt * 512, (tt + 1) * 512)
        for dc in range(NDC):
            k = sqi[0] % 3
            sqi[0] += 1
            P.emit("act", (lambda a, k=k, dc=dc: a.activation(out=sq[k][:, :], in_=h[:, dc, ts], func=AF.Square)),
                   reads=[h_reg[dc][tt]], writes=[sq_reg[k]])
            P.emit("pe", mm(bank[ssq_bank][:, :], onesD[:, :], sq[k][:, :], dc == 0, dc == NDC - 1),
                   reads=[sq_reg[k]], writes=[bank_reg[ssq_bank]])
        P.emit("act", (lambda a: a.activation(out=rt[:, :], in_=bank[ssq_bank][:, :], func=AF.Sqrt, bias=EPS, scale=1.0)),
               writes=[bank_reg[ssq_bank], rt_reg])
        P.emit("dve", (lambda v: v.reciprocal(out=rstd[:, :], in_=rt[:, :])), reads=[rt_reg], writes=[rstd_reg])
        for dc in range(NDC):
            gcol = gains_sb[:, n * NDC + dc:n * NDC + dc + 1]
            P.emit("dve", (lambda v, dc=dc, gcol=gcol: v.scalar_tensor_tensor(
                out=dst_fn(dc), in0=h[:, dc, ts], scalar=gcol, in1=rstd[:, :], op0=ALU.mult, op1=ALU.mult)),
                reads=[h_reg[dc][tt], rstd_reg, const_reg], writes=dst_regs(dc))

    def ffn(n, gain_idx):
        for tt in range(NTT):
            ts = slice(tt * 512, (tt + 1) * 512)
            rmsnorm(gain_idx, tt, (lambda dc, ts=ts: u[:, dc, ts]), (lambda dc, tt=tt: [u_reg[tt]]), 6 + tt % 2)
        k = 0
        kk = 0
        for fh in range(2):
            for fcl in range(FH):
                slab, sreg = next_slab()
                wg = slab[:, 0:1024].rearrange("p (a c) -> p a c", a=NDC)
                wu = slab[:, 1024:2048].rearrange("p (a c) -> p a c", a=NDC)
                for tt in range(NTT):
                    ts = slice(tt * 512, (tt + 1) * 512)
                    gb = k % 2
                    ub = 2 + k % 2
                    si = k % 3
                    k += 1
                    P.emit("pe", [mm(bank[gb][:, :], wg[:, dc, :], u[:, dc, ts], dc == 0, dc == NDC - 1) for dc in range(NDC)],
                           reads=[u_reg[tt], sreg], writes=[bank_reg[gb]])
                    P.emit("pe", [mm(bank[ub][:, :], wu[:, dc, :], u[:, dc, ts], dc == 0, dc == NDC - 1) for dc in range(NDC)],
                           reads=[u_reg[tt], sreg], writes=[bank_reg[ub]])
                    P.emit("act", (lambda a, gb=gb, si=si: a.activation(out=sg[si][:, :], in_=bank[gb][:, :], func=AF.Silu)),
                           writes=[bank_reg[gb], sg_reg[si]])
                    P.emit("dve", (lambda v, ub=ub, si=si, fcl=fcl, ts=ts: v.tensor_tensor(
                        out=A[:, fcl, ts], in0=bank[ub][:, :], in1=sg[si][:, :], op=ALU.mult)),
                        reads=[sg_reg[si]], writes=[bank_reg[ub], A_reg[fcl][tt]])
            for dco in range(NDC):
                slab, sreg = next_slab()
                wdv = slab[:, 0:FH * 128].rearrange("p (a c) -> p a c", a=FH)
                for tt in range(NTT):
                    ts = slice(tt * 512, (tt + 1) * 512)
                    yb = 4 + kk % 2
                    kk += 1
                    P.emit("pe", [mm(bank[yb][:, :], wdv[:, f, :], A[:, f, ts], f == 0, f == FH - 1) for f in range(FH)],
                           reads=[A_reg[f][tt] for f in range(FH)] + [sreg], writes=[bank_reg[yb]])
                    P.emit("dve", (lambda v, yb=yb, dco=dco, ts=ts: v.scalar_tensor_tensor(
                        out=h[:, dco, ts], in0=bank[yb][:, :], scalar=0.5, in1=h[:, dco, ts], op0=ALU.mult, op1=ALU.add)),
                        writes=[bank_reg[yb], h_reg[dco][tt]])

    for tt in range(NTT):
        ts = slice(tt * 512, (tt + 1) * 512)
        P.dma("sp", (lambda q, ts=ts: q.dma_start(out=h[:, :, ts], in_=xT[:, :, ts])), "x%d" % tt,
              writes=[h_reg[dc][tt] for dc in range(NDC)])
    for (dst, src) in (
        (gains_sb[:, :], gains),
        (pscale_sb[:, :], pscale_d),
        (rc_sb[:, :], c_rc),
        (ssum_sb[0:64, :, :], c_ssum.rearrange("p (a c) -> p a c", a=4)),
        (tab33[0:32, :], rpb_d),
        (cbias[:, :], rpb_d[31:32, :].partition_broadcast(128)),
        (onehot_sb[0:33, :], c_onehot),
    ):
        P.dma("sp", (lambda q, dst=dst, src=src: q.dma_start(out=dst, in_=src)), "cst", writes=[const_reg])
    P.emit("dve", (lambda v: v.memset(onesD[:, :], 1.0 / D)), writes=[const_reg])
    P.emit("dve", (lambda v: v.memset(ones64[:, :], 1.0)), writes=[const_reg])
    P.emit("dve", (lambda v: v.memset(tab33[32:33, :], NEG)), writes=[const_reg])
    for e in ("pe", "act", "dve"):
        P.wait_all(e, ["cst"])
    P.barrier(engs=("pe", "act", "dve"))

    fb = 0
    for (c0, c1) in ((0, 512), (512, 1024), (1024, FW)):
        P.emit("pe", mm(bank[fb][0:8, 0:c1 - c0], tab33[0:33, 0:8], onehot_sb[0:33, c0:c1], True, True), writes=[bank_reg[fb]])
        P.emit("act", (lambda a, c0=c0, c1=c1: a.activation(out=fexp_sb[0:8, c0:c1], in_=bank[fb][0:8, 0:c1 - c0], func=AF.Exp)),
               writes=[bank_reg[fb], G_reg])
    P.dma("sp", (lambda q: q.dma_start(out=fd_t.ap(), in_=fexp_sb[0:8, :])), "g1", reads=[G_reg], writes=[G_reg])
    for hh in range(H):
        src = bass.AP(fd_t, hh * FW, [[0, 128], [1, 1151]])
        dst = bass.AP(tsk_t, hh * 128 * TL, [[TL + 1, 128], [1, 1151]])
        P.dma("sp", (lambda q, dst=dst, src=src: q.dma_start(out=dst, in_=src)), "g2", reads=[G_reg], writes=[])
    G2_reg = Reg("G2")
    G2_reg.w = ("g2", P.dcnt["g2"])
    Gh_reg = [Reg("Gh0"), Reg("Gh1")]

    def load_G(hh):
        src = bass.AP(tsk_t, hh * 128 * TL + 127, [[TL, 128], [1, 1024]])
        P.dma("sp", (lambda q: q.dma_start(out=G2[hh % 2][:, :], in_=src)), "gh%d" % (hh % 2),
              reads=[G2_reg], writes=[Gh_reg[hh % 2]])

    def out_h_raw():
        for tt in range(NTT):
            ts = slice(tt * 512, (tt + 1) * 512)
            P.dma("sp", (lambda q, ts=ts: q.dma_start(out=outT[:, :, ts], in_=h[:, :, ts])), "st",
                  reads=[h_reg[dc][tt] for dc in range(NDC)])
        P.wait_all("sp", ["st"])

    def done():
        replay(nc, P)
        return nc

    ffn(0, 0)
    if stop_after == "ffn1":
        out_h_raw()
        return done()

    P.barrier()
    for tt in range(NTT):
        ts = slice(tt * 512, (tt + 1) * 512)
        rmsnorm(1, tt, (lambda dc, ts=ts: u[:, dc, ts]), (lambda dc, tt=tt: [u_reg[tt]]), 6 + tt % 2)

    Va4 = Va.rearrange("p i (h c) -> p i h c", c=65)
    Va_reg = [Reg("Va%d" % i) for i in range(16)]
    s0, sr0 = next_slab()
    s1, sr1 = next_slab()
    wv = [s0.rearrange("p (a c) -> p a c", a=4), s1.rearrange("p (a c) -> p a c", a=4)]
    P.emit("dve", (lambda v: v.memset(Va4[:, :, :, 64:65], 1.0)), writes=Va_reg)
    for i in range(16):
        b = i % 4
        tsl = slice(i * 128, (i + 1) * 128)
        P.emit("pe", [mm(bank[b][:, :], u[:, dc, tsl], wv[dc // 4][:, dc % 4, :], dc == 0, dc == NDC - 1) for dc in range(NDC)],
               reads=[u_reg[i // 4], sr0, sr1], writes=[bank_reg[b]])
        src = bank[b][:, :].rearrange("p (h c) -> p h c", c=64)
        if i % 2 == 0:
            P.emit("act", (lambda a, i=i, src=src: a.activation(out=Va4[:, i, :, 0:64], in_=src, func=AF.Copy)),
                   writes=[bank_reg[b], Va_reg[i]])
        else:
            P.emit("dve", (lambda v, i=i, src=src: v.tensor_copy(out=Va4[:, i, :, 0:64], in_=src)),
                   writes=[bank_reg[b], Va_reg[i]])

    Kb_reg = [Reg("Kb0"), Reg("Kb1")]
    Qz_reg = [Reg("Qz0"), Reg("Qz1")]
    Qm_reg = [[Reg("Qm%d_%d" % (b, ib)) for ib in range(8)] for b in range(2)]
    Qa_reg = [[Reg("Qa%d_%d" % (b, tt)) for tt in range(NTT)] for b in range(2)]
    Ka_reg = [[Reg("Ka%d_%d" % (b, tt)) for tt in range(NTT)] for b in range(2)]
    km_reg = Reg("km")
    df_reg = Reg("df")
    ind_reg = [Reg("ind0"), Reg("ind1")]
    Pt_reg = [Reg("Pt%d" % i) for i in range(6)]
    Et_reg = [Reg("Et%d" % i) for i in range(2)]
    dhi_reg = [Reg("dhi%d" % i) for i in range(2)]
    dlo_reg = [Reg("dlo%d" % i) for i in range(2)]
    rec_reg = [Reg("rec0")]
    stg_reg = [Reg("stg%d" % i) for i in range(2)]
    aT_reg = [[Reg("aT%d_%d" % (hh, m)) for m in range(NTT)] for hh in range(H)]
    for b in range(2):
        P.dma("sp", (lambda q, b=b: q.dma_start(out=KaB[b][64:72, :], in_=c_kblk)), "kb%d" % b, writes=[Kb_reg[b]])
        P.emit("dve", (lambda v, b=b: v.memset(QaB[b][64:72, :], 0.0)), writes=[Qz_reg[b]] + Qm_reg[b])
    cnts = {"s": 0, "p": 0, "e": 0, "f": 0, "g": 0, "o": 0}
    SCALE = HD ** -0.5
    load_G(0)

    def proj_groups(hh):
        b = hh % 2
        slab, sreg = next_slab()
        wq = slab[:, 0:512].rearrange("p (a c) -> p a c", a=NDC)
        wk = slab[:, 512:1024].rearrange("p (a c) -> p a c", a=NDC)
        out = []
        for tt in range(NTT):
            ts = slice(tt * 512, (tt + 1) * 512)
            for (wm, dstb, dreg) in ((wq, QaB[b], Qa_reg[b][tt]), (wk, KaB[b], Ka_reg[b][tt])):
                def grp(wm=wm, dstb=dstb, dreg=dreg, ts=ts, tt=tt):
                    P.emit("pe", [mm(bank[0][0:64, :], wm[:, dc, :], u[:, dc, ts], dc == 0, dc == NDC - 1) for dc in range(NDC)],
                           reads=[u_reg[tt], sreg], writes=[bank_reg[0]])
                    P.emit("dve", (lambda v: v.tensor_copy(out=dstb[0:64, ts], in_=bank[0][0:64, :])),
                           writes=[bank_reg[0], dreg])
                out.append(grp)
        return out

    def attn_head(hh, nxt):
        b = hh % 2
        Qa, Ka = QaB[b], KaB[b]
        if hh + 1 < H:
            load_G(hh + 1)
        Gt = G2[hh % 2]
        Greg = Gh_reg[hh % 2]
        P.emit("dve", (lambda v: v.reduce_sum(out=kmT[0:64, 0:8], in_=Ka[0:64, :].rearrange("p (j k) -> p j k", k=BLK), axis=AX.X)),
               reads=Ka_reg[b], writes=[km_reg])
        P.emit("dve", (lambda v: v.tensor_tensor(
            out=diffT[0:64, :].rearrange("p (a b) -> p a b", b=8),
            in0=kmT[0:64, 0:8].unsqueeze(1).broadcast_to([64, 8, 8]),
            in1=kmT[0:64, 0:8].unsqueeze(2).broadcast_to([64, 8, 8]), op=ALU.subtract)),
            reads=[km_reg], writes=[df_reg])

        def selection():
            for half in range(2):
                qs = slice(1024 + half * 512, 1536 + half * 512)
                P.emit("pe", mm(bank[1][0:64, :], diffT[0:64, 0:64], Qa[0:64, qs], True, True),
                       reads=[df_reg, Qa_reg[b][2 + half]], writes=[bank_reg[1]])
                P.emit("dve", (lambda v, half=half: v.tensor_single_scalar(
                    out=ind[half][0:64, :], in_=bank[1][0:64, :], scalar=0.0, op=ALU.is_gt)),
                    writes=[bank_reg[1], ind_reg[half]])
                for blk in range(2):
                    ib = 4 + half * 2 + blk
                    P.emit("pe", mm(bank[1][0:72, 0:256], ssum_sb[0:64, ib - 4, :], ind[half][0:64, blk * 256:(blk + 1) * 256], True, True),
                           reads=[ind_reg[half]], writes=[bank_reg[1]])
                    P.emit("dve", (lambda v, ib=ib: v.tensor_scalar(
                        out=Qa[64:72, ib * BLK:(ib + 1) * BLK], in0=bank[1][64:72, 0:256],
                        scalar1=2.5, scalar2=NEG, op0=ALU.is_gt, op1=ALU.mult)),
                        writes=[bank_reg[1], Qm_reg[b][ib]])

        tiles = [(m, kt) for m in range(NTT) for kt in range(4 * m + 4)]
        info = {}
        pending = []
        DEPTH = 4
        DELAY = 4
        SRING = (3, 4, 5, 2)

        def s_stage(idx):
            m, kt = tiles[idx]
            q0 = m * 512
            sb = SRING[cnts["s"] % len(SRING)]
            cnts["s"] += 1
            pi = cnts["p"] % 6
            cnts["p"] += 1
            info[idx] = pi
            delta = kt * 128 - q0
            cs = max(0, delta)
            rd = [Ka_reg[b][kt // 4], Kb_reg[b], Qa_reg[b][m]] + ([Qz_reg[b]] if m < 2 else [Qm_reg[b][2 * m], Qm_reg[b][2 * m + 1]])
            P.emit("pe", mm(bank[sb][:, cs:512], Ka[0:72, kt * 128:(kt + 1) * 128], Qa[0:72, q0 + cs:q0 + 512], True, True),
                   reads=rd, writes=[bank_reg[sb]])
            if delta < -128:
                P.emit("act", (lambda a, sb=sb, pi=pi: a.activation(
                    out=Pt[pi][:, :], in_=bank[sb][:, :], func=AF.Exp, bias=cbias[:, hh:hh + 1], scale=SCALE)),
                    writes=[bank_reg[sb], Pt_reg[pi]])
            else:
                ei = cnts["e"] % 2
                cnts["e"] += 1
                goff = 384 - delta
                P.emit("act", (lambda a, sb=sb, ei=ei, cs=cs: a.activation(
                    out=Et[ei][:, cs:512], in_=bank[sb][:, cs:512], func=AF.Exp, scale=SCALE)),
                    writes=[bank_reg[sb], Et_reg[ei]])
                eng = "pool" if cnts["o"] % 2 == 1 else "dve"
                cnts["o"] += 1
                P.emit(eng, (lambda g, ei=ei, pi=pi, goff=goff, cs=cs: g.tensor_tensor(
                    out=Pt[pi][:, cs:512], in0=Et[ei][:, cs:512], in1=Gt[:, goff + cs:goff + 512], op=ALU.mult)),
                    reads=[Et_reg[ei], Greg], writes=[Pt_reg[pi]])

        def pv_stage(idx):
            m, kt = tiles[idx]
            q0 = m * 512
            pi = info[idx]
            ob = 6 + m % 2
            last = 4 * m + 3
            cs = max(0, kt * 128 - q0)
            P.emit("pe", mm(bank[ob][0:65, cs:512], Va4[:, kt, hh, :], Pt[pi][:, cs:512], kt == 0, kt == last),
                   reads=[Va_reg[kt], Pt_reg[pi]], writes=[bank_reg[ob]])
            if kt != last:
                return
            fi = cnts["f"] % 2
            cnts["f"] += 1
            P.emit("dve", (lambda v, ob=ob, fi=fi: v.tensor_copy(out=dhi[fi][64:65, :], in_=bank[ob][64:65, :])),
                   writes=[bank_reg[ob], dhi_reg[fi]])
            P.emit("dve", (lambda v, ob=ob, fi=fi: v.tensor_tensor(
                out=dlo[fi][64:65, :], in0=bank[ob][64:65, :], in1=dhi[fi][64:65, :], op=ALU.subtract)),
                reads=[dhi_reg[fi]], writes=[bank_reg[ob], dlo_reg[fi]])

            def stage_b(ob=ob, fi=fi, m=m, q0=q0):
                P.emit("pe", [mm(bank[1][0:64, :], ones64[64:65, 0:64], dhi[fi][64:65, :], True, False),
                              mm(bank[1][0:64, :], ones64[64:65, 0:64], dlo[fi][64:65, :], False, True)],
                       reads=[dhi_reg[fi], dlo_reg[fi]], writes=[bank_reg[1]])
                P.emit("dve", (lambda v: v.reciprocal(out=rec[0][0:64, :], in_=bank[1][0:64, :])),
                       writes=[bank_reg[1], rec_reg[0]])
                c = hh // 2
                if hh % 2 == 0:
                    P.emit("dve", (lambda v: v.tensor_tensor(
                        out=aT[0:64, c, q0:q0 + 512], in0=bank[ob][0:64, :], in1=rec[0][0:64, :], op=ALU.mult)),
                        reads=[rec_reg[0]], writes=[bank_reg[ob], aT_reg[hh][m]])
                else:
                    gi = cnts["g"] % 2
                    cnts["g"] += 1
                    P.emit("dve", (lambda v: v.tensor_tensor(
                        out=stg[gi][0:64, :], in0=bank[ob][0:64, :], in1=rec[0][0:64, :], op=ALU.mult)),
                        reads=[rec_reg[0]], writes=[bank_reg[ob], stg_reg[gi]])
                    P.dma("sp", (lambda q: q.dma_start(out=aT[64:128, c, q0:q0 + 512], in_=stg[gi][0:64, :])),
                          "stg%d" % gi, reads=[stg_reg[gi]], writes=[aT_reg[hh][m]])

            pending.append([DELAY, stage_b])

        n = len(tiles)
        for idx in range(n + DEPTH):
            if idx == 4:
                selection()
            if idx < n:
                s_stage(idx)
            if idx - DEPTH >= 0:
                pv_stage(idx - DEPTH)
            if nxt and idx >= 6 and (idx - 6) % 4 == 0:
                nxt.pop(0)()
            for pb in list(pending):
                pb[0] -= 1
                if pb[0] <= 0:
                    pending.remove(pb)
                    pb[1]()
        for pb in pending:
            pb[1]()
        del pending[:]
        while nxt:
            nxt.pop(0)()

    cur = proj_groups(0)
    for g in cur:
        g()
    for hh in range(H):
        nxt = proj_groups(hh + 1) if hh + 1 < H else []
        attn_head(hh, nxt)
    if stop_after == "m3":
        for c in range(4):
            for tt in range(NTT):
                ts = slice(tt * 512, (tt + 1) * 512)
                P.emit("act", (lambda a, c=c, ts=ts: a.activation(out=h[:, c, ts], in_=aT[:, c, ts], func=AF.Copy)),
                       reads=[aT_reg[hh][tt] for hh in (2 * c, 2 * c + 1)],
                       writes=[h_reg[c][tt]])
        out_h_raw()
        return done()

    P.barrier(dma_sems=("stg0", "stg1"))
    zf_reg = [Reg("zf%d" % tt) for tt in range(NTT)]
    ta_reg = Reg("ta")
    tb_reg = Reg("tb")
    t16_reg = Reg("t16")
    pT_reg = [[Reg("pT%d_%d" % (g, tt)) for tt in range(NTT)] for g in range(4)]
    zslab = {}

    def z_proj(g):
        slab, sreg = next_slab()
        zslab[g] = (slab, sreg)
        wzv = slab[:, 0:1024].rearrange("p (a c) -> p a c", a=NDC)
        for tt in range(NTT):
            ts = slice(tt * 512, (tt + 1) * 512)
            b = (g % 2) * 4 + tt
            P.emit("pe", [mm(bank[b][:, :], wzv[:, dc, :], u[:, dc, ts], dc == 0, dc == NDC - 1) for dc in range(NDC)],
                   reads=[u_reg[tt], sreg], writes=[bank_reg[b]])

    z_proj(0)
    for g in range(4):
        w = 2 ** (g + 1)
        if g < 3:
            z_proj(g + 1)
        slab, sreg = zslab[g]
        wgg = slab[:, 1024:1152]
        for tt in range(NTT):
            ts = slice(tt * 512, (tt + 1) * 512)
            b = (g % 2) * 4 + tt
            P.emit("act", (lambda a, b=b, ts=ts: a.activation(out=zf[:, ts], in_=bank[b][:, :], func=AF.Copy)),
                   writes=[bank_reg[b], zf_reg[tt]])
        bufs = [(ta, ta_reg, ta_bf), (tb, tb_reg, tb_bf)]
        src, src_regs = zf, list(zf_reg)
        sft = 1
        idx = 0
        while sft < w:
            dst, dreg, _ = bufs[idx % 2]
            P.emit("dve", (lambda v, dst=dst, src=src, sft=sft: v.tensor_tensor(
                out=dst[:, sft:S], in0=src[:, sft:S], in1=src[:, 0:S - sft], op=ALU.add)),
                reads=src_regs, writes=[dreg])
            P.emit("dve", (lambda v, dst=dst, src=src, sft=sft: v.tensor_copy(out=dst[:, 0:sft], in_=src[:, 0:sft])),
                   reads=src_regs, writes=[dreg])
            src, src_regs = dst, [dreg]
            idx += 1
            sft *= 2
        fin, fin_regs = src, src_regs
        _, oreg, pooled = bufs[idx % 2]
        P.emit("dve", (lambda v, fin=fin, pooled=pooled, w=w: v.scalar_tensor_tensor(
            out=pooled[:, :], in0=fin[:, :], scalar=1.0 / w, in1=zf[:, :], op0=ALU.mult, op1=ALU.subtract)),
            reads=fin_regs + zf_reg, writes=[oreg])
        P.emit("dve", (lambda v, fin=fin, w=w: v.tensor_tensor(
            out=tmp16[:, 0:w - 1], in0=fin[:, 0:w - 1], in1=rc_sb[:, 0:w - 1], op=ALU.mult)),
            reads=fin_regs, writes=[t16_reg])
        P.emit("dve", (lambda v, pooled=pooled, w=w: v.tensor_tensor(
            out=pooled[:, 0:w - 1], in0=tmp16[:, 0:w - 1], in1=zf[:, 0:w - 1], op=ALU.subtract)),
            reads=[t16_reg] + zf_reg, writes=[oreg])
        for tt in range(NTT):
            ts = slice(tt * 512, (tt + 1) * 512)
            b = (g % 2) * 4 + tt
            P.emit("pe", mm(bank[b][:, :], wgg, pooled[:, ts], True, True), reads=[oreg, sreg], writes=[bank_reg[b]])
            P.emit("dve", (lambda v, b=b, g=g, ts=ts: v.tensor_single_scalar(
                out=pT[:, g, ts], in_=bank[b][:, :], scalar=pscale_sb[:, g:g + 1], op=ALU.mult)),
                writes=[bank_reg[b], pT_reg[g][tt]])

    P.barrier(dma_sems=("stg0", "stg1"))
    mg_reg = [[Reg("mg%d_%d" % (c, tt)) for tt in range(NTT)] for c in range(NDC)]
    sgm_reg = [Reg("sgm0"), Reg("sgm1")]
    m1_reg = [Reg("m10"), Reg("m11")]
    k4 = 0
    for c in range(NDC):
        slabg, sgr = next_slab()
        gaw = slabg[:, 0:1024].rearrange("p (a c) -> p a c", a=NDC)
        gpw = slabg[:, 1024:2048].rearrange("p (a c) -> p a c", a=NDC)
        slabb, sbr = next_slab()
        waw = slabb[:, 0:512].rearrange("p (a c) -> p a c", a=4)
        wpw = slabb[:, 512:1024].rearrange("p (a c) -> p a c", a=4)
        for tt in range(NTT):
            ts = slice(tt * 512, (tt + 1) * 512)
            b0 = 4 * (k4 % 2)
            k4 += 1
            a_rd = [aT_reg[hh][tt] for hh in range(H)]
            P.emit("pe", [mm(bank[b0][:, :], gaw[:, dc, :], u[:, dc, ts], dc == 0, dc == NDC - 1) for dc in range(NDC)],
                   reads=[u_reg[tt], sgr], writes=[bank_reg[b0]])
            P.emit("pe", [mm(bank[b0 + 1][:, :], waw[:, cc, :], aT[:, cc, ts], cc == 0, cc == 3) for cc in range(4)],
                   reads=a_rd + [sbr], writes=[bank_reg[b0 + 1]])
            P.emit("pe", [mm(bank[b0 + 2][:, :], gpw[:, dc, :], u[:, dc, ts], dc == 0, dc == NDC - 1) for dc in range(NDC)],
                   reads=[u_reg[tt], sgr], writes=[bank_reg[b0 + 2]])
            P.emit("pe", [mm(bank[b0 + 3][:, :], wpw[:, g, :], pT[:, g, ts], g == 0, g == 3) for g in range(4)],
                   reads=[pT_reg[g][tt] for g in range(4)] + [sbr], writes=[bank_reg[b0 + 3]])
            P.emit("act", (lambda a, b0=b0: a.activation(out=sgm[0][:, :], in_=bank[b0][:, :], func=AF.Sigmoid)),
                   writes=[bank_reg[b0], sgm_reg[0]])
            P.emit("act", (lambda a, b0=b0: a.activation(out=sgm[1][:, :], in_=bank[b0 + 2][:, :], func=AF.Sigmoid)),
                   writes=[bank_reg[b0 + 2], sgm_reg[1]])
            P.emit("dve", (lambda v, b0=b0: v.tensor_tensor(out=m1[0][:, :], in0=bank[b0 + 1][:, :], in1=sgm[0][:, :], op=ALU.mult)),
                   reads=[sgm_reg[0]], writes=[bank_reg[b0 + 1], m1_reg[0]])
            P.emit("dve", (lambda v, b0=b0: v.tensor_tensor(out=m1[1][:, :], in0=bank[b0 + 3][:, :], in1=sgm[1][:, :], op=ALU.mult)),
                   reads=[sgm_reg[1]], writes=[bank_reg[b0 + 3], m1_reg[1]])
            P.emit("dve", (lambda v, c=c, ts=ts: v.tensor_tensor(out=merged[:, c, ts], in0=m1[0][:, :], in1=m1[1][:, :], op=ALU.add)),
                   reads=[m1_reg[0], m1_reg[1]], writes=[mg_reg[c][tt]])
    ko = 0
    for co in range(NDC):
        slab, sreg = next_slab()
        wo = slab[:, 0:1024].rearrange("p (a c) -> p a c", a=NDC)
        for tt in range(NTT):
            ts = slice(tt * 512, (tt + 1) * 512)
            b = ko % 2
            ko += 1
            P.emit("pe", [mm(bank[b][:, :], wo[:, c, :], merged[:, c, ts], c == 0, c == NDC - 1) for c in range(NDC)],
                   reads=[mg_reg[c][tt] for c in range(NDC)] + [sreg], writes=[bank_reg[b]])
            P.emit("dve", (lambda v, b=b, co=co, ts=ts: v.tensor_tensor(out=h[:, co, ts], in0=bank[b][:, :], in1=h[:, co, ts], op=ALU.add)),
                   writes=[bank_reg[b], h_reg[co][tt]])
    if stop_after == "m4":
        out_h_raw()
        return done()

    P.barrier()
    ffn(1, 2)
    P.barrier()
    of_reg = [Reg("of0"), Reg("of1")]
    for tt in range(NTT):
        ts = slice(tt * 512, (tt + 1) * 512)
        o = of32[tt % 2]
        rmsnorm(3, tt, (lambda dc, o=o: o[:, dc, :]), (lambda dc, tt=tt: [of_reg[tt % 2]]), 6 + tt % 2)
        P.dma("sp", (lambda q, o=o, ts=ts: q.dma_start(out=outT[:, :, ts], in_=o[:, :, :])), "st%d" % (tt % 2),
              reads=[of_reg[tt % 2]])
    P.wait_all("sp", ["st0", "st1"])
    return done()


def replay(nc, P):
    with ExitStack() as es:
        sems = {name: es.enter_context(nc.semaphore(name)) for name in sorted(P.sem_names)}
        block = es.enter_context(nc.Block())

        def run(handle, lst):
            for item in lst:
                if item[0] == "wait":
                    handle.wait_ge(sems[item[1]], item[2])
                else:
                    ins = item[1](handle)
                    if item[2] is not None:
                        ins.then_inc(sems[item[2]], item[3])

        @block.tensor
        def _(t):
            run(t, P.lists["pe"])

        @block.scalar
        def _(a):
            run(a, P.lists["act"])

        @block.vector
        def _(v):
            run(v, P.lists["dve"])

        @block.gpsimd
        def _(g):
            run(g, P.lists["pool"])

        @block.sync
        def _(q):
            run(q, P.lists["sp"])


def _prep_shared(inp):
    f = lambda a: np.ascontiguousarray(np.asarray(a, dtype=np.float32))
    out = {}

    def gu(W):
        return W.reshape(NDC, 128, NFC, 128).transpose(2, 1, 0, 3)

    for n, pre in ((1, "ffn1"), (2, "ffn2")):
        wg = f(inp[pre + "_w_gate"])[0]
        wu = f(inp[pre + "_w_up"])[0]
        wdn = f(inp[pre + "_w_down"])[0]
        out["wgu%d" % n] = f(np.stack([gu(wg), gu(wu)], axis=2).reshape(NFC, 128, 2048))
        out["wd%d" % n] = f(wdn.reshape(2, FH, 128, NDC, 128).transpose(0, 3, 2, 1, 4).reshape(16, 128, FH * 128))
    win = f(inp["w_in"])[0]
    out["wv"] = f(win[:, 1024:1536].reshape(2, 4, 128, 512).transpose(0, 2, 1, 3).reshape(2, 128, 2048))
    out["wz"] = f(win[:, 1536:2048].reshape(NDC, 128, 4, 128).transpose(2, 1, 0, 3).reshape(4, 128, 1024))
    out["wgrp"] = f(inp["pool_w_group"])[0]
    out["wqk"] = f(win[:, 0:1024].reshape(NDC, 128, 2, H, HD).transpose(3, 1, 2, 0, 4).reshape(H, 128, 1024))
    out["wgate"] = f(win[:, 2048:4096].reshape(NDC, 128, 2, 8, 128).transpose(3, 1, 2, 0, 4).reshape(8, 128, 2048))
    wab = np.stack([f(inp["w_branch_attn"])[0], f(inp["w_branch_pool"])[0]], axis=0)
    out["wbr"] = f(wab.reshape(2, 4, 128, 8, 128).transpose(3, 2, 0, 1, 4).reshape(8, 128, 1024))
    out["wout"] = f(f(inp["w_out"])[0].reshape(NDC, 128, 8, 128).transpose(2, 1, 0, 3).reshape(8, 128, 1024))
    g = np.stack([f(inp["ffn1_norm"])[0], f(inp["mix_norm"])[0], f(inp["ffn2_norm"])[0], f(inp["final_norm"])], axis=0)
    out["gains"] = f(g.reshape(4, NDC, 128).transpose(2, 0, 1).reshape(128, 4 * NDC))
    out["pscale"] = f(f(inp["pool_scale"])[0].reshape(4, 128).T)
    out["rpb"] = f(inp["rpb_table"])
    out.update(_host_consts())
    return out


_CACHE = {}


def _run(inputs, stop_after="all", ncores=NB):
    if stop_after not in _CACHE:
        _CACHE[stop_after] = build_program(stop_after)
    nc = _CACHE[stop_after]
    shared = _prep_shared(inputs)
    x = np.asarray(inputs["x"], dtype=np.float32)
    in_maps = []
    for b in range(ncores):
        m = dict(shared)
        m["xT"] = np.ascontiguousarray(x[b].reshape(S, NDC, 128).transpose(2, 1, 0))
        in_maps.append(m)
    res = run_bass_kernel_spmd(nc, in_maps, core_ids=list(range(ncores)))
    out = np.zeros((NB, S, D), np.float32)
    for b in range(ncores):
        out[b] = np.asarray(res.results[b]["outT"]).transpose(2, 1, 0).reshape(S, D)
    return out


def kernel(**inputs):
    return _run(inputs, "all")
```
